# Optimizing a Trainium2 kernel written in Bass

```python
import jax, jax.numpy as jnp
from jax import lax
import numpy as np

D_MODEL = 2048
BATCH = 1
SEQ = 8192
DEPTH = 1

MIX_WIDTH = D_MODEL
CHUNK = 128
SGU_HEADS = 8
SGU_WIDTH = MIX_WIDTH // 2
SGU_HEAD_DIM = SGU_WIDTH // SGU_HEADS
RET_HEADS = 8
RET_WIDTH = MIX_WIDTH - SGU_WIDTH
RET_HEAD_DIM = RET_WIDTH // RET_HEADS
IN_WIDTH = 2 * SGU_WIDTH + 4 * RET_WIDTH
D_FF = ((8 * D_MODEL // 3 + 255) // 256) * 256
ROPE_BASE = 10000.0
EPS = 1e-6

kernel_name = "hybrid_sgu_retention_block"


def _rms(x, g):
    xf = x.astype(jnp.float32)
    y = xf * lax.rsqrt(jnp.mean(xf * xf, axis=-1, keepdims=True) + EPS)
    return (y * g.astype(jnp.float32)).astype(x.dtype)


def _rotary(x, pos):
    half = x.shape[-1] // 2
    inv = 1.0 / (ROPE_BASE ** (jnp.arange(half, dtype=jnp.float32) / half))
    ang = pos.astype(jnp.float32)[:, None] * inv[None, :]
    cos = jnp.cos(ang)[None, :, None, :]
    sin = jnp.sin(ang)[None, :, None, :]
    x1, x2 = x[..., :half], x[..., half:]
    return jnp.concatenate([x1 * cos - x2 * sin, x1 * sin + x2 * cos], axis=-1)


def _spatial_gate(u, v, ln_g, ln_b, w_s, b_s):
    B, S, _ = u.shape
    nc = S // CHUNK
    u = u.reshape(B, nc, CHUNK, SGU_HEADS, SGU_HEAD_DIM).astype(jnp.float32)
    v = v.reshape(B, nc, CHUNK, SGU_HEADS, SGU_HEAD_DIM).astype(jnp.float32)
    mu = jnp.mean(v, axis=-1, keepdims=True)
    var = jnp.mean(jnp.square(v - mu), axis=-1, keepdims=True)
    vn = (v - mu) * lax.rsqrt(var + EPS) * ln_g.astype(jnp.float32) + ln_b.astype(jnp.float32)
    causal = jnp.tril(jnp.ones((CHUNK, CHUNK), dtype=jnp.float32))
    ws = w_s.astype(jnp.float32) * causal[None]
    mixed = jnp.einsum('hts,bnshd->bnthd', ws, vn) + b_s.astype(jnp.float32).T[:, :, None]
    return (u * mixed).reshape(B, S, SGU_WIDTH)


def _retention(q, k, v, g, gn_g, gn_b, pos):
    B, S, _ = q.shape
    nc = S // CHUNK
    H, Dk = RET_HEADS, RET_HEAD_DIM
    q = _rotary(q.astype(jnp.float32).reshape(B, S, H, Dk), pos)
    k = _rotary(k.astype(jnp.float32).reshape(B, S, H, Dk), pos) * (Dk ** -0.5)
    q = q.reshape(B, nc, CHUNK, H, Dk)
    k = k.reshape(B, nc, CHUNK, H, Dk)
    v = v.astype(jnp.float32).reshape(B, nc, CHUNK, H, Dk)

    log_gamma = jnp.log(1.0 - jnp.exp2(-5.0 - jnp.arange(H, dtype=jnp.float32)))
    idx = jnp.arange(CHUNK, dtype=jnp.float32)
    diff = idx[:, None] - idx[None, :]
    decay = jnp.where(diff[None] >= 0,
                      jnp.exp(jnp.maximum(diff, 0.0)[None] * log_gamma[:, None, None]), 0.0)
    zeta = jnp.exp((CHUNK - 1.0 - idx)[None, :] * log_gamma[:, None])
    xi = jnp.exp((idx + 1.0)[None, :] * log_gamma[:, None])
    gamma_c = jnp.exp(CHUNK * log_gamma)

    scores = jnp.einsum('bnihd,bnjhd->bnhij', q, k) * decay[None, None]
    intra = jnp.einsum('bnhij,bnjhe->bnihe', scores, v)

    kv_chunk = jnp.einsum('bnjhd,bnjhe,hj->nbhde', k, v, zeta)

    def step(R, kv):
        return gamma_c[None, :, None, None] * R + kv, R

    _, R_prev = lax.scan(step, jnp.zeros((B, H, Dk, Dk), jnp.float32), kv_chunk)
    inter = jnp.einsum('bnihd,nbhde,hi->bnihe', q, R_prev, xi)

    o = intra + inter
    mu = jnp.mean(o, axis=-1, keepdims=True)
    var = jnp.mean(jnp.square(o - mu), axis=-1, keepdims=True)
    o = (o - mu) * lax.rsqrt(var + EPS) * gn_g.astype(jnp.float32) + gn_b.astype(jnp.float32)
    o = o.reshape(B, S, RET_WIDTH)
    return jax.nn.silu(g.astype(jnp.float32)) * o


def setup_inputs(seed: int = 0) -> dict:
    key = jax.random.key(seed)
    ks = jax.random.split(key, 18)
    f = jnp.float32
    n = lambda k, shape, s: jax.random.normal(k, shape, f) * s
    return {
        "x": jax.random.normal(ks[0], (BATCH, SEQ, D_MODEL), f),
        "norm1_g": 1.0 + n(ks[1], (DEPTH, D_MODEL), 0.02),
        "w_in": n(ks[2], (DEPTH, D_MODEL, IN_WIDTH), D_MODEL ** -0.5),
        "sgu_ln_g": 1.0 + n(ks[3], (DEPTH, SGU_HEADS, SGU_HEAD_DIM), 0.02),
        "sgu_ln_b": n(ks[4], (DEPTH, SGU_HEADS, SGU_HEAD_DIM), 0.02),
        "w_spatial": n(ks[5], (DEPTH, SGU_HEADS, CHUNK, CHUNK), CHUNK ** -0.5),
        "b_spatial": 1.0 + n(ks[6], (DEPTH, SGU_HEADS, CHUNK), 0.02),
        "ret_gn_g": 1.0 + n(ks[7], (DEPTH, RET_HEADS, RET_HEAD_DIM), 0.02),
        "ret_gn_b": n(ks[8], (DEPTH, RET_HEADS, RET_HEAD_DIM), 0.02),
        "w_out": n(ks[9], (DEPTH, MIX_WIDTH, D_MODEL), MIX_WIDTH ** -0.5),
        "norm2_g": 1.0 + n(ks[10], (DEPTH, D_MODEL), 0.02),
        "w_gate": n(ks[11], (DEPTH, D_MODEL, D_FF), D_MODEL ** -0.5),
        "w_up": n(ks[12], (DEPTH, D_MODEL, D_FF), D_MODEL ** -0.5),
        "w_down": n(ks[13], (DEPTH, D_FF, D_MODEL), D_FF ** -0.5),
        "final_norm_g": 1.0 + n(ks[14], (D_MODEL,), 0.02),
    }


def reference(x, norm1_g, w_in, sgu_ln_g, sgu_ln_b, w_spatial, b_spatial, ret_gn_g, ret_gn_b,
              w_out, norm2_g, w_gate, w_up, w_down, final_norm_g):
    S = x.shape[1]
    pos = jnp.arange(S, dtype=jnp.int32)
    splits = [SGU_WIDTH, 2 * SGU_WIDTH, 2 * SGU_WIDTH + RET_WIDTH,
              2 * SGU_WIDTH + 2 * RET_WIDTH, 2 * SGU_WIDTH + 3 * RET_WIDTH]
    for l in range(DEPTH):
        h = _rms(x, norm1_g[l])
        proj = jnp.einsum('bsd,de->bse', h, w_in[l])
        u, v_s, q, k, v_r, g = jnp.split(proj, splits, axis=-1)
        y_sgu = _spatial_gate(u, v_s, sgu_ln_g[l], sgu_ln_b[l], w_spatial[l], b_spatial[l])
        y_ret = _retention(q, k, v_r, g, ret_gn_g[l], ret_gn_b[l], pos)
        mixed = jnp.concatenate([y_sgu, y_ret], axis=-1).astype(x.dtype)
        x = x + jnp.einsum('bse,ed->bsd', mixed, w_out[l])
        h2 = _rms(x, norm2_g[l])
        a = jax.nn.silu(jnp.einsum('bsd,df->bsf', h2, w_gate[l])) * jnp.einsum('bsd,df->bsf', h2, w_up[l])
        x = x + jnp.einsum('bsf,fd->bsd', a, w_down[l])
    return _rms(x, final_norm_g)
```

```python
import contextlib
import numpy as np
import ml_dtypes
import concourse.bass as bass
import concourse.mybir as mybir
from concourse.bass_utils import run_bass_kernel_spmd

F32 = mybir.dt.float32
BF16 = mybir.dt.bfloat16
AF = mybir.ActivationFunctionType
ALU = mybir.AluOpType
AX = mybir.AxisListType

NCORES = 8
SEQ = 8192
D = 2048
TPC = SEQ // NCORES
NT = TPC // 128
DC = D // 128
INW = 6144
DFF = 5632
NFB = DFF // 512
EPS = 1e-6
OFF_U, OFF_VS, OFF_Q, OFF_K, OFF_V, OFF_G = 0, 1024, 2048, 3072, 4096, 5120


def I(name, *args, **kw):
    return lambda e: getattr(e, name)(*args, **kw)


class Eng:
    def __init__(self, name, sem):
        self.name = name
        self.sem = sem
        self.ops = []
        self.cnt = 0
        self.seen = {}

    def wait(self, toks):
        best = {}
        for t in toks:
            if t is None:
                continue
            if t[2] not in best or best[t[2]][1] < t[1]:
                best[t[2]] = t
        for sem, val, key in best.values():
            if self.seen.get(key, 0) >= val:
                continue
            self.seen[key] = val
            self.ops.append(I("wait_ge", sem, val))

    def emit(self, fn, sig=True):
        if sig:
            self.cnt += 1
            c = self.cnt
            sem = self.sem
            self.ops.append(lambda e, fn=fn, sem=sem: fn(e).then_inc(sem, 1))
            return (self.sem, c, self.name)
        self.ops.append(lambda e, fn=fn: fn(e))
        return None


class Buf:
    def __init__(self, name):
        self.name = name
        self.w = {}
        self.r = {}
        self.dsem = None
        self.dcnt = 0

    def wtoks(self):
        return list(self.w.values())

    def rtoks(self):
        return list(self.r.values())

    def alltoks(self):
        return self.wtoks() + self.rtoks()

    def add_read(self, t):
        k = t[2]
        if k not in self.r or self.r[k][1] < t[1]:
            self.r[k] = t

    def add_write(self, t):
        self.w = {t[2]: t}
        self.r = {}


class Prog:
    def __init__(self, nc, es):
        self.nc = nc
        self.es = es
        self.nsem = 0
        self.act = Eng("act", self.sem("s_act"))
        self.dve = Eng("dve", self.sem("s_dve"))
        self.pool = Eng("pool", self.sem("s_pool"))
        self.pe = Eng("pe", self.sem("s_pe"))
        self.sp = Eng("sp", self.sem("s_sp"))
        self.bufs = []

    def sem(self, name):
        self.nsem += 1
        return self.es.enter_context(self.nc.semaphore(name))

    def buf(self, name):
        b = Buf(name)
        self.bufs.append(b)
        return b

    def _deps(self, reads, writes, extra):
        toks = list(extra)
        for b in reads:
            toks += b.wtoks()
        for b in writes:
            toks += b.wtoks() + b.rtoks()
        return toks

    def op(self, eng, fn, reads=(), writes=(), extra=()):
        eng.wait(self._deps(reads, writes, extra))
        t = eng.emit(fn, sig=True)
        for b in reads:
            b.add_read(t)
        for b in writes:
            b.add_write(t)
        return t

    def pegroup(self, fns, reads=(), writes=(), extra=()):
        return self.group(self.pe, fns, reads, writes, extra)

    def group(self, eng, fns, reads=(), writes=(), extra=()):
        eng.wait(self._deps(reads, writes, extra))
        for fn in fns[:-1]:
            eng.emit(fn, sig=False)
        t = eng.emit(fns[-1], sig=True)
        for b in reads:
            b.add_read(t)
        for b in writes:
            b.add_write(t)
        return t

    def dma(self, q, out, in_, owner, reads=(), writes=(), extra=()):
        if owner.dsem is None:
            owner.dsem = self.sem("d_" + owner.name)
        q.wait(self._deps(reads, writes, extra))
        owner.dcnt += 1
        sem, val = owner.dsem, 16 * owner.dcnt
        q.ops.append(lambda e, out=out, in_=in_, sem=sem: e.dma_start(out=out, in_=in_).then_inc(sem, 16))
        t = (sem, val, "d_" + owner.name)
        for b in reads:
            b.add_read(t)
        for b in writes:
            b.add_write(t)
        return t


def _log_gamma():
    h = np.arange(8, dtype=np.float64)
    return np.log(1.0 - np.exp2(-5.0 - h))


def _const_tables():
    lg = _log_gamma()
    t = np.arange(128, dtype=np.float64)
    s = 128.0 ** -0.5
    zeta = s * np.exp((128.0 - t)[:, None] * lg[None, :])
    gneg = s * np.exp(-t[:, None] * lg[None, :])
    eps2 = EPS * np.exp(-2.0 * t[:, None] * lg[None, :])
    gamma_c = np.exp(128.0 * lg)
    mask = (t[:, None] <= t[None, :]).astype(np.float32)
    ident = np.eye(128, dtype=np.float32)
    return zeta, gneg, eps2, gamma_c, mask, ident


def _core_tables(core):
    lg = _log_gamma()
    half = 64
    inv = 1.0 / (10000.0 ** (np.arange(half, dtype=np.float64) / half))
    pos = core * TPC + np.arange(TPC, dtype=np.float64)
    ang = pos[:, None] * inv[None, :]
    cos = np.cos(ang).reshape(NT, 128, half).transpose(1, 0, 2)
    sin = np.sin(ang).reshape(NT, 128, half).transpose(1, 0, 2)
    coef = np.zeros((8, 8), dtype=np.float64)
    for cp in range(core):
        coef[cp, :] = np.exp(1024.0 * (core - 1 - cp) * lg)
    return (np.ascontiguousarray(cos, dtype=np.float32), np.ascontiguousarray(sin, dtype=np.float32),
            coef.reshape(1, 64).astype(np.float32))


def build_program(stop_after=None, dumps=(), nocc=False):
    nc = bass.Bass("TRN2", target_bir_lowering=False)
    es = contextlib.ExitStack()
    _, _, _, gamma_c, _, _ = _const_tables()
    gamma_c = [float(np.float32(v)) for v in gamma_c]

    def din(name, shape, dt=F32):
        return nc.dram_tensor(name, shape, dt, kind="ExternalInput").ap()

    x = din("x", [TPC, D])
    w_in = din("w_in", [D, INW])
    w_out = din("w_out", [D, D])
    lite = stop_after is not None and stop_after not in ("ffn1",)
    w_gate = din("w_gate", [D, DFF] if not lite else [128, 128])
    w_up = din("w_up", [D, DFF] if not lite else [128, 128])
    w_down = din("w_down", [DFF, D] if not lite else [128, 128])
    n1g = din("n1g", [1, D])
    n2g = din("n2g", [1, D])
    nfg = din("nfg", [1, D])
    lnt_d = din("lnt", [4, 1024])
    wsp_d = din("wsp", [8, 128, 128])
    tabs_d = din("tabs", [128, 32])
    cos_d = din("cos", [128, NT, 64])
    sin_d = din("sin", [128, NT, 64])
    coef_d = din("coef", [1, 64])
    ident_d = din("ident", [128, 128], BF16)
    mask_d = din("mask", [128, 128], BF16)
    out = nc.dram_tensor("out", [TPC, D], F32, kind="ExternalOutput").ap()
    ib = [nc.dram_tensor(f"ib{g}", [128, 512], F32) for g in range(2)]
    ob = [nc.dram_tensor(f"ob{g}", [NCORES * 128, 512], F32) for g in range(2)]
    dump_out = {}

    w_in_v = w_in.rearrange("(c p) e -> p c e", p=128)
    w_out_v = w_out.rearrange("(c p) e -> p c e", p=128)
    w_gate_v = w_gate.rearrange("(c p) f -> p c f", p=128)
    w_up_v = w_up.rearrange("(c p) f -> p c f", p=128)
    w_down_v = w_down.rearrange("(c p) e -> p c e", p=128)

    def sb(name, shape, dt):
        return es.enter_context(nc.sbuf_tensor("s_" + name, shape, dt))

    P = Prog(nc, es)
    act, dve, pool, pe, sp = P.act, P.dve, P.pool, P.pe, P.sp

    ARA = sb("arena_a", [128, 32768], BF16)
    HT = sb("ht", [128, DC, TPC], BF16)
    WBT = [sb(f"wb{i}", [128, 8192], BF16) for i in range(2)]
    ARB = sb("arena_b", [128, 39680], BF16)
    IDENT = sb("ident", [128, 128], BF16)
    MASK = sb("maskt", [128, 128], BF16)
    STAT = sb("stat", [128, 24], F32)
    RSTD = sb("rstd", [128, 24], F32)
    NHALF = sb("nhalf", [128, 8], F32)
    ST4 = sb("st4", [128, 8, 8], F32)

    def carve(arena, off, n, dt=BF16):
        if dt == BF16:
            return arena[:, off:off + n], off + n
        assert dt == F32
        return arena[:, off:off + 2 * n].bitcast(F32), off + 2 * n

    o = 0
    MIXED_f, o = carve(ARA, o, NT * 2048)
    MIXED = MIXED_f.rearrange("p (m d) -> p m d", m=NT)
    KZ_f, o = carve(ARA, o, NT * 512)
    KZ = KZ_f.rearrange("p (m d) -> p m d", m=NT)
    QT_f, o = carve(ARA, o, 4 * TPC)
    QT = QT_f.rearrange("p (h t) -> p h t", h=4)
    G_f, o = carve(ARA, o, NT * 512)
    G = G_f.rearrange("p (m d) -> p m d", m=NT)
    V_f, o = carve(ARA, o, NT * 512)
    V = V_f.rearrange("p (m d) -> p m d", m=NT)
    assert o == 32768
    o = 0
    XIN = []
    for i in range(2):
        a, o = carve(ARA, o, 2048, F32)
        XIN.append(a)
    GB1, o = carve(ARA, o, 2048, F32)
    HB = []
    for i in range(2):
        a, o = carve(ARA, o, 2048)
        HB.append(a)
    X1 = ARA[:, :].bitcast(F32).rearrange("p (m d) -> p m d", m=NT)

    o = 0
    KT_f, o = carve(ARB, o, 4 * TPC)
    KT = KT_f.rearrange("p (h t) -> p h t", h=4)
    VN_f, o = carve(ARB, o, NT * 512)
    VN = VN_f.rearrange("p (m d) -> p m d", m=NT)
    MB = []
    for i in range(2):
        a, o = carve(ARB, o, 512, F32)
        MB.append(a.rearrange("p (h d) -> p h d", h=4))
    COS_f, o = carve(ARB, o, NT * 64, F32)
    COS = COS_f.rearrange("p (m d) -> p m d", m=NT)
    SIN_f, o = carve(ARB, o, NT * 64, F32)
    SIN = SIN_f.rearrange("p (m d) -> p m d", m=NT)
    LNT_f, o = carve(ARB, o, 4096, F32)
    LNT = LNT_f.rearrange("p (k d) -> p k d", k=4)
    WST_f, o = carve(ARB, o, 1024)
    WST = WST_f.rearrange("p (h t) -> p h t", h=8)
    WSP_f, o = carve(ARB, o, 1024)
    WSP = WSP_f.rearrange("p (h s) -> p h s", h=8)
    TABS, o = carve(ARB, o, 32, F32)
    COEF, o = carve(ARB, o, 64, F32)
    RST_f, o = carve(ARB, o, 512, F32)
    RST = RST_f.rearrange("p (h d) -> p h d", h=4)
    RN = []
    for i in range(2):
        a, o = carve(ARB, o, 512)
        RN.append(a.rearrange("p (h d) -> p h d", h=4))
    AGL = []
    for i in range(2):
        a, o = carve(ARB, o, 2 * 512, F32)
        AGL.append(a.rearrange("p (r h d) -> p r h d", r=2, h=4))
    TT = []
    for i in range(2):
        tl = []
        for j in range(4):
            a, o = carve(ARB, o, 256, F32)
            tl.append(a.rearrange("p (h d) -> p h d", h=4))
        TT.append(tl)
    KR32_f, o = carve(ARB, o, 512, F32)
    KR32 = KR32_f.rearrange("p (h r d) -> p h r d", h=4, r=2)
    KRB = []
    for i in range(3):
        a, o = carve(ARB, o, 512)
        KRB.append(a)
    SQ_f, o = carve(ARB, o, 512, F32)
    SQ = SQ_f.rearrange("p (h d) -> p h d", h=4)
    Z_f, o = carve(ARB, o, 512, F32)
    Z = Z_f.rearrange("p (h d) -> p h d", h=4)
    STB = []
    for i in range(2):
        a, o = carve(ARB, o, 512)
        STB.append(a.rearrange("p (h d) -> p h d", h=4))
    assert o <= 39680, o
    o = 0
    WD = []
    for i in range(2):
        a, o = carve(ARB, o, 8192)
        WD.append(a.rearrange("p (c e) -> p c e", c=4))
    AT = []
    for i in range(2):
        a, o = carve(ARB, o, 4096)
        AT.append(a.rearrange("p (c t) -> p c t", c=4))
    SG = []
    for i in range(2):
        a, o = carve(ARB, o, 512)
        SG.append(a)
    HB2 = []
    for i in range(2):
        a, o = carve(ARB, o, 2048)
        HB2.append(a)
    GB2, o = carve(ARB, o, 2048, F32)
    assert o <= 39680, o

    WB = [w[:, :].rearrange("p (c e) -> p c e", c=DC) for w in WBT]
    WG = [w[:, 0:4096].rearrange("p (c e) -> p c e", c=DC) for w in WBT]
    WU = [w[:, 4096:8192].rearrange("p (c e) -> p c e", c=DC) for w in WBT]

    BANK = [es.enter_context(nc.psum_tensor(f"bank{i}", [128, 512], F32)) for i in range(8)]
    bBANK = [P.buf(f"bank{i}") for i in range(8)]

    def bank4(i):
        return BANK[i][:, :].rearrange("p (h d) -> p h d", h=4)

    def bank_bf(i):
        return BANK[i][:, :].bitcast(BF16).rearrange("p (g d) -> p g d", g=8)

    bTAB = P.buf("tab")
    bXIN = [P.buf(f"xin{i}") for i in range(2)]
    bHB = [P.buf(f"hb{i}") for i in range(2)]
    bGB = P.buf("gb")
    bHT = [P.buf(f"ht{m}") for m in range(NT)]
    bWB = [P.buf(f"wb{i}") for i in range(2)]
    bSTAT = P.buf("stat")
    bSTc = [P.buf(f"statc{i}") for i in range(24)]
    bV = [P.buf(f"v{m}") for m in range(NT)]
    bKZ = [P.buf(f"kz{m}") for m in range(NT)]
    bKT = [P.buf(f"kt{m}") for m in range(NT)]
    bQT = [P.buf(f"qt{m}") for m in range(NT)]
    bG = [P.buf(f"g{m}") for m in range(NT)]
    bVN = [P.buf(f"vn{m}") for m in range(NT)]
    bMB = [P.buf(f"mb{i}") for i in range(2)]
    bMIX = [P.buf(f"mix{m}") for m in range(NT)]
    bRST = P.buf("rst")
    bRN = [P.buf(f"rn{i}") for i in range(2)]
    bAGL = [P.buf(f"agl{i}") for i in range(2)]
    bTT = [P.buf(f"tt{i}") for i in range(2)]
    bKR32 = P.buf("kr32")
    bKRB = [P.buf(f"krb{i}") for i in range(3)]
    bSQ = P.buf("sq")
    bZ = P.buf("z")
    bSTB = [P.buf(f"stb{i}") for i in range(2)]
    bST4 = [P.buf(f"st4_{i}") for i in range(8)]
    bWST = P.buf("wst")
    bWSP = P.buf("wsp")
    bIB = [P.buf(f"ib{g}") for g in range(2)]
    bOB = [P.buf(f"ob{g}") for g in range(2)]
    bX1 = [P.buf(f"x1_{m}") for m in range(NT)]
    bWD = [P.buf(f"wd{i}") for i in range(2)]
    bAT = [P.buf(f"at{i}") for i in range(2)]
    bSG = [P.buf(f"sg{i}") for i in range(2)]
    bWGU = [P.buf(f"wgu{i}") for i in range(2)]
    bHB2 = [P.buf(f"hb2_{i}") for i in range(2)]
    bGB2 = P.buf("gb2")
    phaseA_bufs = list(P.bufs)

    def tab_load(dst, src, q=None):
        P.dma(q or sp, dst, src, owner=bTAB, writes=[])
    n_tab = 0
    for dst, src in ((IDENT[:], ident_d[:, :]), (MASK[:], mask_d[:, :]), (COS_f, cos_d.rearrange("p m d -> p (m d)")),
                     (SIN_f, sin_d.rearrange("p m d -> p (m d)")), (TABS, tabs_d[:, :]),
                     (COEF, coef_d[0:1, :].broadcast_to([128, 64]))):
        tab_load(dst, src)
        n_tab += 1
    for k in range(4):
        tab_load(LNT[:, k, :], lnt_d[k:k + 1, :].broadcast_to([128, 1024]))
        n_tab += 1
    tab_load(GB1, n1g[0:1, :].broadcast_to([128, D]))
    n_tab += 1
    TABTOK = (bTAB.dsem, 16 * n_tab, "d_tab")
    P.dma(pool, WSP, wsp_d.rearrange("h t s -> t h s"), owner=bWSP, writes=[bWSP])
    P.op(pool, I("memset", STAT[:], 0.0), writes=[bSTAT])
    tNH = P.op(pool, I("memset", NHALF[:], -0.5))
    EPSC = sb("epsc", [128, 4], F32)
    tEPS = P.op(pool, I("memset", EPSC[:], EPS))
    P.pegroup([I("transpose", out=bank_bf(3)[:, h, :], in_=WSP[:, h, :], identity=IDENT[:]) for h in range(8)],
              reads=[bWSP], writes=[bBANK[3]], extra=[TABTOK])
    P.op(dve, I("tensor_tensor", out=WST, in0=bank_bf(3), in1=MASK[:].unsqueeze(1).broadcast_to([128, 8, 128]),
                                        op=ALU.mult), reads=[bBANK[3]], writes=[bWST], extra=[TABTOK])

    class _Stop(Exception):
        pass

    def chk(name):
        if stop_after == name:
            raise _Stop()

    wq = {"n": 0}

    def load_wblock(src_ap):
        slot = wq["n"] % 2
        wq["n"] += 1
        P.dma(pool, WB[slot], src_ap, owner=bWB[slot], writes=[bWB[slot]])
        return slot

    pj = {"n": 0}

    def next_pj():
        b = pj["n"] % 3
        pj["n"] += 1
        return b

    def proj_group(slot, m, bank):
        fns = [I("matmul", BANK[bank][:, :], lhsT=HT[:, c, m * 128:(m + 1) * 128], rhs=WB[slot][:, c, :],
                                       start=(c == 0), stop=(c == DC - 1)) for c in range(DC)]
        return P.pegroup(fns, reads=[bHT[m], bWB[slot]], writes=[bBANK[bank]])

    def rms_stats(src_ap, src_buf, junk_ap, junk_buf, col):
        P.op(act, I("activation", out=junk_ap, in_=src_ap, func=AF.Square, accum_out=STAT[:, col:col + 1]),
             reads=[src_buf, bSTAT], writes=[junk_buf, bSTc[col]])
        P.op(dve, I("tensor_scalar", out=RSTD[:, col:col + 1], in0=STAT[:, col:col + 1], scalar1=1.0 / D,
                                            scalar2=EPS, op0=ALU.mult, op1=ALU.add), reads=[bSTc[col]], writes=[bSTc[col]])
        P.op(pool, I("tensor_tensor", out=RSTD[:, col:col + 1], in0=RSTD[:, col:col + 1], in1=NHALF[:, 0:1],
                                             op=ALU.pow), reads=[bSTc[col]], writes=[bSTc[col]], extra=[tNH])

    def transposes_to_HT(src_ap, src_buf, m):
        for half in range(2):
            bk = 3 + half
            P.pegroup([I("transpose", out=bank_bf(bk)[:, c % 8, :], in_=src_ap[:, c * 128:(c + 1) * 128],
                                                          identity=IDENT[:]) for c in range(half * 8, half * 8 + 8)],
                      reads=[src_buf], writes=[bBANK[bk]], extra=[TABTOK])
            P.op(act, I("activation", out=HT[:, half * 8:half * 8 + 8, m * 128:(m + 1) * 128],
                                                               in_=bank_bf(bk), func=AF.Copy),
                 reads=[bBANK[bk]], writes=[bHT[m]])

    st4 = {"n": 0}

    def group_stats(bank, eps_ap):
        s = st4["n"] % 8
        st4["n"] += 1
        b = bST4[s]
        chk("vs_a")
        P.op(pool, I("memset", ST4[:, s, :], 0.0), writes=[b])
        fns = [I("activation", out=Z[:, h, :], in_=bank4(bank)[:, h, :], func=AF.Identity,
                 accum_out=ST4[:, s, h:h + 1]) for h in range(4)]
        fns += [I("activation", out=SQ[:, h, :], in_=bank4(bank)[:, h, :], func=AF.Square,
                  accum_out=ST4[:, s, 4 + h:5 + h]) for h in range(4)]
        P.group(act, fns, reads=[bBANK[bank]], writes=[bSQ, bZ, b])
        chk("vs_c")
        P.op(dve, I("tensor_scalar", out=ST4[:, s, 0:4], in0=ST4[:, s, 0:4], scalar1=1.0 / 128, scalar2=None,
                                            op0=ALU.mult), reads=[b], writes=[b])
        P.op(dve, I("scalar_tensor_tensor", out=ST4[:, s, 4:8], in0=ST4[:, s, 4:8], scalar=1.0 / 128,
                                                   in1=eps_ap, op0=ALU.mult, op1=ALU.add), reads=[b], writes=[b],
             extra=[TABTOK])
        P.op(dve, I("tensor_tensor", out=SQ_f[:, 0:4], in0=ST4[:, s, 0:4], in1=ST4[:, s, 0:4], op=ALU.mult),
             reads=[b], writes=[bSQ])
        P.op(dve, I("tensor_tensor", out=ST4[:, s, 4:8], in0=ST4[:, s, 4:8], in1=SQ_f[:, 0:4], op=ALU.subtract),
             reads=[b, bSQ], writes=[b])
        chk("vs_d")
        P.op(pool, I("tensor_tensor", out=ST4[:, s, 4:8], in0=ST4[:, s, 4:8], in1=NHALF[:, 0:4], op=ALU.pow),
             reads=[b], writes=[b], extra=[tNH])
        chk("vs_e")
        return s, b

    def rotary(bank, m, tslot):
        pv = BANK[bank][:, :].rearrange("p (h r d) -> p h r d", h=4, r=2)
        x1, x2 = pv[:, :, 0, :], pv[:, :, 1, :]
        cb = COS[:, m, :].unsqueeze(1).broadcast_to([128, 4, 64])
        sbb = SIN[:, m, :].unsqueeze(1).broadcast_to([128, 4, 64])
        T = TT[tslot]
        bT = bTT[tslot]
        P.group(dve, [I("tensor_tensor", out=T[j], in0=a, in1=b_, op=ALU.mult)
                      for j, (a, b_) in enumerate(((x1, cb), (x2, sbb), (x1, sbb), (x2, cb)))],
                reads=[bBANK[bank]], writes=[bT], extra=[TABTOK])
        P.group(pool, [I("tensor_tensor", out=KR32[:, :, 0, :], in0=T[0], in1=T[1], op=ALU.subtract),
                       I("tensor_tensor", out=KR32[:, :, 1, :], in0=T[2], in1=T[3], op=ALU.add)],
                reads=[bT], writes=[bKR32])

    for m in range(NT):
        i = m % 2
        P.dma(sp, XIN[i], x[m * 128:(m + 1) * 128, :], owner=bXIN[i], writes=[bXIN[i]])
        rms_stats(XIN[i], bXIN[i], HB[i], bHB[i], m)
        P.op(dve, I("scalar_tensor_tensor", out=HB[i], in0=XIN[i], scalar=RSTD[:, m:m + 1], in1=GB1,
                                                             op0=ALU.mult, op1=ALU.mult),
             reads=[bXIN[i], bSTc[m]], writes=[bHB[i]], extra=[TABTOK])
        transposes_to_HT(HB[i], bHB[i], m)

    done = {"flag": stop_after == "norm1"}

    def mixers(grp):
        hs = [4 * grp + h for h in range(4)]
        slot = load_wblock(w_in_v[:, :, OFF_V + 512 * grp: OFF_V + 512 * grp + 512])
        for m in range(NT):
            bk = next_pj()
            proj_group(slot, m, bk)
            P.op(act, I("activation", out=V[:, m, :], in_=BANK[bk][:, :], func=AF.Copy),
                 reads=[bBANK[bk]], writes=[bV[m]])
        chk("v")
        slot = load_wblock(w_in_v[:, :, OFF_K + 512 * grp: OFF_K + 512 * grp + 512])
        P.op(pool, I("memset", RST_f, 0.0), writes=[bRST])

        def k_post(m):
            ks = m % 3
            P.pegroup([I("transpose", out=bank_bf(3)[:, h, :], in_=KRB[ks][:, h * 128:(h + 1) * 128],
                                                          identity=IDENT[:]) for h in range(4)],
                      reads=[bKRB[ks]], writes=[bBANK[3]], extra=[TABTOK])
            P.op(act, I("activation", out=KT[:, :, m * 128:(m + 1) * 128], in_=bank_bf(3)[:, 0:4, :],
                                                  func=AF.Copy), reads=[bBANK[3]], writes=[bKT[m]])
            state_update(m)

        def state_update(m):
            P.pegroup([I("matmul", bank4(5)[:, h, :], lhsT=KZ[:, m, h * 128:(h + 1) * 128],
                                                    rhs=V[:, m, h * 128:(h + 1) * 128], start=True, stop=True)
                       for h in range(4)], reads=[bKZ[m], bV[m]], writes=[bBANK[5]])
            P.group(dve, [I("scalar_tensor_tensor", out=RST[:, h, :], in0=RST[:, h, :], scalar=gamma_c[hs[h]],
                                                                 in1=bank4(5)[:, h, :], op0=ALU.mult, op1=ALU.add)
                          for h in range(4)], reads=[bBANK[5]], writes=[bRST])

        for m in range(NT):
            bk = next_pj()
            proj_group(slot, m, bk)
            rotary(bk, m, m % 2)
            ks = m % 3
            P.op(act, I("activation", out=KRB[ks], in_=KR32_f, func=AF.Copy), reads=[bKR32], writes=[bKRB[ks]])
            P.op(pool, I("tensor_tensor", out=KZ[:, m, :].rearrange("p (h d) -> p h d", h=4),
                                                      in0=KR32_f.rearrange("p (h d) -> p h d", h=4),
                                                      in1=TABS[:, hs[0]:hs[0] + 4].unsqueeze(2).broadcast_to([128, 4, 128]),
                                                      op=ALU.mult), reads=[bKR32], writes=[bKZ[m]], extra=[TABTOK])
            if m >= 2:
                k_post(m - 2)
        k_post(NT - 2)
        k_post(NT - 1)
        chk("k")
        P.dma(sp, ib[grp].ap()[:, :], RST_f, owner=bIB[grp], reads=[bRST], writes=[bIB[grp]])
        ccsem = P.sem(f"cc{grp}")
        if not nocc:
            pool.wait(bIB[grp].wtoks() + bOB[grp].alltoks())
            pool.ops.append(lambda e, grp=grp, ccsem=ccsem: e.collective_compute(
                "AllGather", ALU.bypass, replica_groups=[list(range(NCORES))],
                ins=[ib[grp].ap().opt()], outs=[ob[grp].ap().opt()]).then_inc(ccsem))
            tcc = (ccsem, 1, f"cc{grp}")
            bIB[grp].add_read(tcc)
            bOB[grp].add_write(tcc)
        chk("ag")
        slot = load_wblock(w_in_v[:, :, OFF_VS + 512 * grp: OFF_VS + 512 * grp + 512])
        lng = LNT[:, 0, 512 * grp:512 * grp + 512].rearrange("p (h d) -> p h d", h=4)
        lnb = LNT[:, 1, 512 * grp:512 * grp + 512].rearrange("p (h d) -> p h d", h=4)
        gng = LNT[:, 2, 512 * grp:512 * grp + 512].rearrange("p (h d) -> p h d", h=4)
        gnb = LNT[:, 3, 512 * grp:512 * grp + 512].rearrange("p (h d) -> p h d", h=4)

        def normalize(bank, s, b, extra=()):
            P.group(dve, [I("tensor_scalar", out=Z[:, h, :], in0=Z[:, h, :],
                                                          scalar1=ST4[:, s, h:h + 1], scalar2=ST4[:, s, 4 + h:5 + h],
                                                          op0=ALU.subtract, op1=ALU.mult) for h in range(4)],
                    reads=[b], writes=[bZ])

        for m in range(NT):
            bk = next_pj()
            proj_group(slot, m, bk)
            s, b = group_stats(bk, EPSC[:])
            dve.wait([tEPS])
            normalize(bk, s, b)
            chk("vs_f")
            P.op(dve, I("tensor_tensor", out=Z, in0=Z, in1=lng, op=ALU.mult), reads=[bZ], writes=[bZ], extra=[TABTOK])
            P.op(dve, I("tensor_tensor", out=VN[:, m, :].rearrange("p (h d) -> p h d", h=4), in0=Z, in1=lnb,
                                                     op=ALU.add), reads=[bZ], writes=[bVN[m]])
        chk("vs")
        slot = load_wblock(w_in_v[:, :, OFF_U + 512 * grp: OFF_U + 512 * grp + 512])
        for m in range(NT):
            i = m % 2
            P.pegroup([I("matmul", bank4(6)[:, h, :], lhsT=WST[:, hs[h], :],
                                                    rhs=VN[:, m, h * 128:(h + 1) * 128], start=True, stop=True)
                       for h in range(4)], reads=[bWST, bVN[m]], writes=[bBANK[6]])
            P.group(act, [I("activation", out=MB[i][:, h, :], in_=bank4(6)[:, h, :], func=AF.Identity,
                                                            bias=TABS[:, 24 + hs[h]:25 + hs[h]], scale=1.0)
                          for h in range(4)], reads=[bBANK[6]], writes=[bMB[i]], extra=[TABTOK])
            bk = next_pj()
            proj_group(slot, m, bk)
            P.op(dve, I("tensor_tensor",
                out=MIXED[:, m, 512 * grp:512 * grp + 512], in0=BANK[bk][:, :],
                in1=MB[i].rearrange("p h d -> p (h d)"), op=ALU.mult), reads=[bBANK[bk], bMB[i]], writes=[bMIX[m]])
        chk("sgu")
        slot = load_wblock(w_in_v[:, :, OFF_G + 512 * grp: OFF_G + 512 * grp + 512])
        for m in range(NT):
            bk = next_pj()
            proj_group(slot, m, bk)
            P.op(act, I("activation", out=G[:, m, :], in_=BANK[bk][:, :], func=AF.Silu),
                 reads=[bBANK[bk]], writes=[bG[m]])
        chk("g")
        slot = load_wblock(w_in_v[:, :, OFF_Q + 512 * grp: OFF_Q + 512 * grp + 512])

        def q_post(m):
            ks = m % 3
            P.pegroup([I("transpose", out=bank_bf(4)[:, h, :], in_=KRB[ks][:, h * 128:(h + 1) * 128],
                                                          identity=IDENT[:]) for h in range(4)],
                      reads=[bKRB[ks]], writes=[bBANK[4]], extra=[TABTOK])
            P.op(act, I("activation", out=QT[:, :, m * 128:(m + 1) * 128], in_=bank_bf(4)[:, 0:4, :],
                                                  func=AF.Copy), reads=[bBANK[4]], writes=[bQT[m]])

        for m in range(NT):
            bk = next_pj()
            proj_group(slot, m, bk)
            rotary(bk, m, m % 2)
            ks = m % 3
            P.op(act, I("activation", out=KRB[ks], in_=KR32_f, func=AF.Copy), reads=[bKR32], writes=[bKRB[ks]])
            if m >= 2:
                q_post(m - 2)
        q_post(NT - 2)
        q_post(NT - 1)
        chk("q")
        for piece in range(4):
            half = piece % 2
            if nocc:
                P.dma(sp, AGL[half][:, 0].rearrange("p h d -> p (h d)"), ib[grp].ap()[:, :],
                      owner=bAGL[half], reads=[bIB[grp]], writes=[bAGL[half]])
                P.dma(sp, AGL[half][:, 1].rearrange("p h d -> p (h d)"), ib[grp].ap()[:, :],
                      owner=bAGL[half], reads=[bIB[grp]], writes=[bAGL[half]])
            else:
                P.dma(sp, AGL[half].rearrange("p r h d -> p r (h d)"),
                      ob[grp].ap()[piece * 256:(piece + 1) * 256, :].rearrange("(r p) f -> p r f", p=128),
                      owner=bAGL[half], reads=[bOB[grp]], writes=[bAGL[half]])
            for r in range(2):
                cp = piece * 2 + r
                if cp == 0:
                    fns = [I("tensor_scalar",
                        out=RST[:, h, :], in0=AGL[half][:, r, h, :], scalar1=COEF[:, hs[h]:hs[h] + 1], scalar2=None,
                        op0=ALU.mult) for h in range(4)]
                else:
                    fns = [I("scalar_tensor_tensor",
                        out=RST[:, h, :], in0=AGL[half][:, r, h, :], scalar=COEF[:, cp * 8 + hs[h]:cp * 8 + hs[h] + 1],
                        in1=RST[:, h, :], op0=ALU.mult, op1=ALU.add) for h in range(4)]
                P.group(dve, fns, reads=[bAGL[half]], writes=[bRST], extra=[TABTOK])
        chk("comb")
        for n in range(NT):
            i = n % 2
            P.op(act, I("activation", out=RN[i].rearrange("p h d -> p (h d)"), in_=RST_f, func=AF.Copy),
                 reads=[bRST], writes=[bRN[i]])
            P.pegroup([I("matmul", bank4(6)[:, h, :], lhsT=KT[:, h, n * 128:(n + 1) * 128],
                                                    rhs=QT[:, h, n * 128:(n + 1) * 128], start=True, stop=True)
                       for h in range(4)], reads=[bKT[n], bQT[n]], writes=[bBANK[6]])
            P.group(dve, [I("scalar_tensor_tensor",
                out=STB[i][:, h, :], in0=bank4(6)[:, h, :], scalar=TABS[:, 8 + hs[h]:9 + hs[h]], in1=MASK[:],
                op0=ALU.mult, op1=ALU.mult) for h in range(4)], reads=[bBANK[6]], writes=[bSTB[i]], extra=[TABTOK])
            fns = []
            for h in range(4):
                fns.append(I("matmul", bank4(7)[:, h, :], lhsT=STB[i][:, h, :],
                                                             rhs=V[:, n, h * 128:(h + 1) * 128], start=True, stop=False))
                fns.append(I("matmul", bank4(7)[:, h, :], lhsT=QT[:, h, n * 128:(n + 1) * 128],
                                                             rhs=RN[i][:, h, :], start=False, stop=True))
            P.pegroup(fns, reads=[bSTB[i], bV[n], bQT[n], bRN[i]], writes=[bBANK[7]])
            s, b = group_stats(7, TABS[:, 16 + hs[0]:16 + hs[0] + 4])
            normalize(7, s, b)
            P.op(dve, I("tensor_tensor", out=Z, in0=Z, in1=gng, op=ALU.mult), reads=[bZ], writes=[bZ], extra=[TABTOK])
            P.op(dve, I("tensor_tensor", out=Z, in0=Z, in1=gnb, op=ALU.add), reads=[bZ], writes=[bZ])
            P.op(dve, I("tensor_tensor",
                out=MIXED[:, n, 1024 + 512 * grp:1024 + 512 * grp + 512], in0=Z_f, in1=G[:, n, :], op=ALU.mult),
                reads=[bZ, bG[n]], writes=[bMIX[n]])
            if n < NT - 1:
                state_update(n)

    try:
        for grp in range((1 if stop_after == "mix0" else 2) if not done["flag"] else 0):
            mixers(grp)
    except _Stop:
        done["flag"] = True

    if stop_after in ("mix", "mix0"):
        done["flag"] = True

    if not done["flag"]:
        for m in range(NT):
            transposes_to_HT(MIXED[:, m, :], bMIX[m], m)
        phaseA_toks = []
        for b in phaseA_bufs:
            phaseA_toks += b.alltoks()
        for m in range(NT):
            P.dma(sp, X1[:, m, :], x[m * 128:(m + 1) * 128, :], owner=bX1[m], writes=[bX1[m]], extra=phaseA_toks)
        P.dma(sp, GB2, n2g[0:1, :].broadcast_to([128, D]), owner=bGB2, writes=[bGB2], extra=phaseA_toks)
        for cb in range(4):
            slot = load_wblock(w_out_v[:, :, cb * 512:(cb + 1) * 512])
            for m in range(NT):
                bk = next_pj()
                proj_group(slot, m, bk)
                P.op(dve, I("tensor_tensor",
                    out=X1[:, m, cb * 512:(cb + 1) * 512], in0=BANK[bk][:, :], in1=X1[:, m, cb * 512:(cb + 1) * 512],
                    op=ALU.add), reads=[bBANK[bk], bX1[m]], writes=[bX1[m]])
    if stop_after == "oproj":
        done["flag"] = True

    if not done["flag"]:
        for eng_ in (act, dve, pool):
            eng_.wait(phaseA_toks)
        for m in range(NT):
            i = m % 2
            rms_stats(X1[:, m, :], bX1[m], HB2[i], bHB2[i], 8 + m)
            P.op(dve, I("scalar_tensor_tensor", out=HB2[i], in0=X1[:, m, :], scalar=RSTD[:, 8 + m:9 + m],
                                                                 in1=GB2, op0=ALU.mult, op1=ALU.mult),
                 reads=[bX1[m], bSTc[8 + m], bGB2], writes=[bHB2[i]])
            transposes_to_HT(HB2[i], bHB2[i], m)
        ffn_first = list(phaseA_toks)
        for b in bWB:
            ffn_first += b.alltoks()
        gq = {"n": 0}

        def load_gu(fb, sbk):
            q = gq["n"]
            gq["n"] += 1
            slot = q % 2
            f0 = fb * 512 + sbk * 256
            P.dma(pool, WG[slot], w_gate_v[:, :, f0:f0 + 256], owner=bWGU[slot], writes=[bWGU[slot]], extra=ffn_first)
            P.dma(pool, WU[slot], w_up_v[:, :, f0:f0 + 256], owner=bWGU[slot], writes=[], extra=ffn_first)
            t = (bWGU[slot].dsem, 16 * bWGU[slot].dcnt, "d_" + bWGU[slot].name)
            bWGU[slot].add_write(t)
            return slot

        def load_wd(fb):
            slot = fb % 2
            P.dma(pool, WD[slot], w_down_v[:, 4 * fb:4 * fb + 4, :], owner=bWD[slot], writes=[bWD[slot]], extra=ffn_first)

        gu_pair = {"n": 0}

        def gu_units(fb):
            for sbk in range(2):
                wslot = load_gu(fb, sbk)
                for j in range(2):
                    fc = sbk * 2 + j
                    for hh in range(2):
                        pr = gu_pair["n"] % 3
                        gu_pair["n"] += 1
                        bg, bu = 2 * pr, 2 * pr + 1
                        fns = [I("matmul",
                            BANK[bg][:, :], lhsT=WG[wslot][:, c, j * 128:(j + 1) * 128],
                            rhs=HT[:, c, hh * 512:(hh + 1) * 512], start=(c == 0), stop=(c == DC - 1)) for c in range(DC)]
                        fns += [I("matmul",
                            BANK[bu][:, :], lhsT=WU[wslot][:, c, j * 128:(j + 1) * 128],
                            rhs=HT[:, c, hh * 512:(hh + 1) * 512], start=(c == 0), stop=(c == DC - 1)) for c in range(DC)]
                        P.pegroup(fns, reads=[bWGU[wslot]] + [bHT[4 * hh + k] for k in range(4)],
                                  writes=[bBANK[bg], bBANK[bu]])
                        si = gu_pair["n"] % 2
                        P.op(act, I("activation", out=SG[si], in_=BANK[bg][:, :], func=AF.Silu),
                             reads=[bBANK[bg]], writes=[bSG[si]], extra=phaseA_toks)
                        first = (fc == 0 and hh == 0)
                        t = P.op(dve, I("tensor_tensor",
                            out=AT[fb % 2][:, fc, hh * 512:(hh + 1) * 512], in0=BANK[bu][:, :], in1=SG[si], op=ALU.mult),
                            reads=[bBANK[bu], bSG[si]], writes=[bAT[fb % 2]] if first else [])
                        if not first:
                            bAT[fb % 2].add_write(t)

        dn_bank = {"n": 0}

        def dn_units(fb):
            a = fb % 2
            for m in range(NT):
                for cb in range(4):
                    bk = 6 + dn_bank["n"] % 2
                    dn_bank["n"] += 1
                    fns = [I("matmul",
                        BANK[bk][:, :], lhsT=AT[a][:, fc, m * 128:(m + 1) * 128], rhs=WD[a][:, fc, cb * 512:(cb + 1) * 512],
                        start=(fc == 0), stop=(fc == 3)) for fc in range(4)]
                    P.pegroup(fns, reads=[bAT[a], bWD[a]], writes=[bBANK[bk]])
                    P.op(dve, I("tensor_tensor",
                        out=X1[:, m, cb * 512:(cb + 1) * 512], in0=BANK[bk][:, :], in1=X1[:, m, cb * 512:(cb + 1) * 512],
                        op=ALU.add), reads=[bBANK[bk], bX1[m]], writes=[bX1[m]])

        nfb = NFB if stop_after != "ffn1" else 1
        load_wd(0)
        gu_units(0)
        for fb in range(1, nfb):
            load_wd(fb)
            gu_units(fb)
            dn_units(fb - 1)
        dn_units(nfb - 1)
        P.dma(sp, GB2, nfg[0:1, :].broadcast_to([128, D]), owner=bGB2, writes=[bGB2])
        for m in range(NT):
            i = m % 2
            rms_stats(X1[:, m, :], bX1[m], HB2[i], bHB2[i], 16 + m)
            P.op(dve, I("scalar_tensor_tensor", out=X1[:, m, :], in0=X1[:, m, :], scalar=RSTD[:, 16 + m:17 + m],
                                                            in1=GB2, op0=ALU.mult, op1=ALU.mult),
                 reads=[bX1[m], bSTc[16 + m], bGB2], writes=[bX1[m]])
            P.dma(sp, out[m * 128:(m + 1) * 128, :], X1[:, m, :], owner=bX1[m], reads=[bX1[m]])

    for name, getter, shape, dt in dumps:
        dten = nc.dram_tensor("dump_" + name, shape, dt, kind="ExternalOutput").ap()
        ap = getter(locals())
        bd = P.buf("dump_" + name)
        alltoks = []
        for b in P.bufs:
            alltoks += b.alltoks()
        P.dma(sp, dten, ap, owner=bd, extra=alltoks, reads=[bd])

    final = []
    for b in P.bufs:
        final += b.alltoks()
    sp.wait(final)

    with nc.Block() as block:
        for ename, q in (("sync", sp), ("scalar", act), ("vector", dve), ("gpsimd", pool), ("tensor", pe)):
            def body(e, q=q):
                for fn in q.ops:
                    fn(e)
            getattr(block, ename)(body)
    es.close()
    return nc


def make_in_maps(x, norm1_g, w_in, sgu_ln_g, sgu_ln_b, w_spatial, b_spatial, ret_gn_g, ret_gn_b,
                 w_out, norm2_g, w_gate, w_up, w_down, final_norm_g):
    f32 = lambda a: np.ascontiguousarray(np.asarray(a), dtype=np.float32)
    zeta, gneg, eps2, _, mask, ident = _const_tables()
    tabs = np.concatenate([zeta, gneg, eps2, f32(b_spatial)[0].T.astype(np.float64)], axis=1).astype(np.float32)
    shared = {
        "w_in": f32(w_in)[0], "w_out": f32(w_out)[0], "w_gate": f32(w_gate)[0], "w_up": f32(w_up)[0],
        "w_down": f32(w_down)[0],
        "n1g": f32(norm1_g).reshape(1, D), "n2g": f32(norm2_g).reshape(1, D), "nfg": f32(final_norm_g).reshape(1, D),
        "lnt": np.stack([f32(sgu_ln_g).reshape(-1), f32(sgu_ln_b).reshape(-1), f32(ret_gn_g).reshape(-1),
                         f32(ret_gn_b).reshape(-1)], axis=0),
        "wsp": f32(w_spatial)[0],
        "tabs": np.ascontiguousarray(tabs),
        "ident": ident.astype(ml_dtypes.bfloat16), "mask": mask.astype(ml_dtypes.bfloat16),
    }
    xf = f32(x)[0]
    in_maps = []
    for c in range(NCORES):
        cos, sin, coef = _core_tables(c)
        m = dict(shared)
        m["x"] = np.ascontiguousarray(xf[c * TPC:(c + 1) * TPC])
        m["cos"], m["sin"], m["coef"] = cos, sin, coef
        in_maps.append(m)
    return in_maps


_NC_CACHE = {}


def kernel(**inputs):
    in_maps = make_in_maps(**inputs)
    if "nc" not in _NC_CACHE:
        _NC_CACHE["nc"] = build_program()
    res = run_bass_kernel_spmd(_NC_CACHE["nc"], in_maps, core_ids=list(range(NCORES)))
    return np.concatenate([r["out"] for r in res.results], axis=0).reshape(1, SEQ, D).astype(np.float32)
```

```python
import contextlib
import numpy as np
import ml_dtypes
import concourse.bass as bass
import concourse.mybir as mybir
from concourse.bass_utils import run_bass_kernel_spmd

F32 = mybir.dt.float32
BF16 = mybir.dt.bfloat16
AF = mybir.ActivationFunctionType
ALU = mybir.AluOpType
AX = mybir.AxisListType

NCORES = 8
SEQ = 8192
D = 2048
TPC = SEQ // NCORES
NT = TPC // 128
DC = D // 128
INW = 6144
DFF = 5632
NFB = DFF // 512
EPS = 1e-6
OFF_U, OFF_VS, OFF_Q, OFF_K, OFF_V, OFF_G = 0, 1024, 2048, 3072, 4096, 5120


def I(name, *args, **kw):
    return lambda e: getattr(e, name)(*args, **kw)


class Eng:
    def __init__(self, name, sem):
        self.name = name
        self.sem = sem
        self.ops = []
        self.cnt = 0
        self.seen = {}

    def wait(self, toks):
        best = {}
        for t in toks:
            if t is None:
                continue
            if t[2] not in best or best[t[2]][1] < t[1]:
                best[t[2]] = t
        for sem, val, key in best.values():
            if self.seen.get(key, 0) >= val:
                continue
            self.seen[key] = val
            self.ops.append(I("wait_ge", sem, val))

    def emit(self, fn, sig=True):
        if sig:
            self.cnt += 1
            c = self.cnt
            sem = self.sem
            self.ops.append(lambda e, fn=fn, sem=sem: fn(e).then_inc(sem, 1))
            return (self.sem, c, self.name)
        self.ops.append(lambda e, fn=fn: fn(e))
        return None


class Buf:
    def __init__(self, name):
        self.name = name
        self.w = {}
        self.r = {}
        self.dsem = None
        self.dcnt = 0

    def wtoks(self):
        return list(self.w.values())

    def rtoks(self):
        return list(self.r.values())

    def alltoks(self):
        return self.wtoks() + self.rtoks()

    def add_read(self, t):
        k = t[2]
        if k not in self.r or self.r[k][1] < t[1]:
            self.r[k] = t

    def add_write(self, t):
        self.w = {t[2]: t}
        self.r = {}


class Prog:
    def __init__(self, nc, es):
        self.nc = nc
        self.es = es
        self.nsem = 0
        self.act = Eng("act", self.sem("s_act"))
        self.dve = Eng("dve", self.sem("s_dve"))
        self.pool = Eng("pool", self.sem("s_pool"))
        self.pe = Eng("pe", self.sem("s_pe"))
        self.sp = Eng("sp", self.sem("s_sp"))
        self.bufs = []

    def sem(self, name):
        self.nsem += 1
        return self.es.enter_context(self.nc.semaphore(name))

    def buf(self, name):
        b = Buf(name)
        self.bufs.append(b)
        return b

    def _deps(self, reads, writes, extra):
        toks = list(extra)
        for b in reads:
            toks += b.wtoks()
        for b in writes:
            toks += b.wtoks() + b.rtoks()
        return toks

    def op(self, eng, fn, reads=(), writes=(), extra=()):
        eng.wait(self._deps(reads, writes, extra))
        t = eng.emit(fn, sig=True)
        for b in reads:
            b.add_read(t)
        for b in writes:
            b.add_write(t)
        return t

    def pegroup(self, fns, reads=(), writes=(), extra=()):
        return self.group(self.pe, fns, reads, writes, extra)

    def group(self, eng, fns, reads=(), writes=(), extra=()):
        eng.wait(self._deps(reads, writes, extra))
        for fn in fns[:-1]:
            eng.emit(fn, sig=False)
        t = eng.emit(fns[-1], sig=True)
        for b in reads:
            b.add_read(t)
        for b in writes:
            b.add_write(t)
        return t

    def dma(self, q, out, in_, owner, reads=(), writes=(), extra=()):
        if owner.dsem is None:
            owner.dsem = self.sem("d_" + owner.name)
        q.wait(self._deps(reads, writes, extra))
        owner.dcnt += 1
        sem, val = owner.dsem, 16 * owner.dcnt
        q.ops.append(lambda e, out=out, in_=in_, sem=sem: e.dma_start(out=out, in_=in_).then_inc(sem, 16))
        t = (sem, val, "d_" + owner.name)
        for b in reads:
            b.add_read(t)
        for b in writes:
            b.add_write(t)
        return t


def _log_gamma():
    h = np.arange(8, dtype=np.float64)
    return np.log(1.0 - np.exp2(-5.0 - h))


def _const_tables():
    lg = _log_gamma()
    t = np.arange(128, dtype=np.float64)
    s = 128.0 ** -0.5
    zeta = s * np.exp((128.0 - t)[:, None] * lg[None, :])
    gneg = s * np.exp(-t[:, None] * lg[None, :])
    eps2 = EPS * np.exp(-2.0 * t[:, None] * lg[None, :])
    gamma_c = np.exp(128.0 * lg)
    mask = (t[:, None] <= t[None, :]).astype(np.float32)
    ident = np.eye(128, dtype=np.float32)
    return zeta, gneg, eps2, gamma_c, mask, ident


def _core_tables(core):
    lg = _log_gamma()
    half = 64
    inv = 1.0 / (10000.0 ** (np.arange(half, dtype=np.float64) / half))
    pos = core * TPC + np.arange(TPC, dtype=np.float64)
    ang = pos[:, None] * inv[None, :]
    cos = np.cos(ang).reshape(NT, 128, half).transpose(1, 0, 2)
    sin = np.sin(ang).reshape(NT, 128, half).transpose(1, 0, 2)
    coef = np.zeros((8, 8), dtype=np.float64)
    for cp in range(core):
        coef[cp, :] = np.exp(1024.0 * (core - 1 - cp) * lg)
    return (np.ascontiguousarray(cos, dtype=np.float32), np.ascontiguousarray(sin, dtype=np.float32),
            coef.reshape(1, 64).astype(np.float32))


def build_program(stop_after=None, dumps=(), nocc=False):
    nc = bass.Bass("TRN2", target_bir_lowering=False)
    es = contextlib.ExitStack()
    _, _, _, gamma_c, _, _ = _const_tables()
    gamma_c = [float(np.float32(v)) for v in gamma_c]

    def din(name, shape, dt=F32):
        return nc.dram_tensor(name, shape, dt, kind="ExternalInput").ap()

    x = din("x", [TPC, D])
    w_in = din("w_in", [D, INW])
    w_out = din("w_out", [D, D])
    lite = stop_after is not None and stop_after not in ("ffn1",)
    w_gate = din("w_gate", [D, DFF] if not lite else [128, 128])
    w_up = din("w_up", [D, DFF] if not lite else [128, 128])
    w_down = din("w_down", [DFF, D] if not lite else [128, 128])
    n1g = din("n1g", [1, D])
    n2g = din("n2g", [1, D])
    nfg = din("nfg", [1, D])
    lnt_d = din("lnt", [4, 1024])
    wsp_d = din("wsp", [8, 128, 128])
    tabs_d = din("tabs", [128, 32])
    cos_d = din("cos", [128, NT, 64])
    sin_d = din("sin", [128, NT, 64])
    coef_d = din("coef", [1, 64])
    ident_d = din("ident", [128, 128], BF16)
    mask_d = din("mask", [128, 128], BF16)
    out = nc.dram_tensor("out", [TPC, D], F32, kind="ExternalOutput").ap()
    ib = [nc.dram_tensor(f"ib{g}", [128, 512], F32) for g in range(2)]
    ob = [nc.dram_tensor(f"ob{g}", [NCORES * 128, 512], F32) for g in range(2)]
    dump_out = {}

    w_in_v = w_in.rearrange("(c p) e -> p c e", p=128)
    w_out_v = w_out.rearrange("(c p) e -> p c e", p=128)
    w_gate_v = w_gate.rearrange("(c p) f -> p c f", p=128)
    w_up_v = w_up.rearrange("(c p) f -> p c f", p=128)
    w_down_v = w_down.rearrange("(c p) e -> p c e", p=128)

    def sb(name, shape, dt):
        return es.enter_context(nc.sbuf_tensor("s_" + name, shape, dt))

    P = Prog(nc, es)
    act, dve, pool, pe, sp = P.act, P.dve, P.pool, P.pe, P.sp

    ARA = sb("arena_a", [128, 32768], BF16)
    HT = sb("ht", [128, DC, TPC], BF16)
    WBT = [sb(f"wb{i}", [128, 8192], BF16) for i in range(2)]
    ARB = sb("arena_b", [128, 39680], BF16)
    IDENT = sb("ident", [128, 128], BF16)
    MASK = sb("maskt", [128, 128], BF16)
    STAT = sb("stat", [128, 24], F32)
    RSTD = sb("rstd", [128, 24], F32)
    NHALF = sb("nhalf", [128, 8], F32)
    ST4 = sb("st4", [128, 8, 8], F32)

    def carve(arena, off, n, dt=BF16):
        if dt == BF16:
            return arena[:, off:off + n], off + n
        assert dt == F32
        return arena[:, off:off + 2 * n].bitcast(F32), off + 2 * n

    o = 0
    MIXED_f, o = carve(ARA, o, NT * 2048)
    MIXED = MIXED_f.rearrange("p (m d) -> p m d", m=NT)
    KZ_f, o = carve(ARA, o, NT * 512)
    KZ = KZ_f.rearrange("p (m d) -> p m d", m=NT)
    QT_f, o = carve(ARA, o, 4 * TPC)
    QT = QT_f.rearrange("p (h t) -> p h t", h=4)
    G_f, o = carve(ARA, o, NT * 512)
    G = G_f.rearrange("p (m d) -> p m d", m=NT)
    V_f, o = carve(ARA, o, NT * 512)
    V = V_f.rearrange("p (m d) -> p m d", m=NT)
    assert o == 32768
    o = 0
    XIN = []
    for i in range(2):
        a, o = carve(ARA, o, 2048, F32)
        XIN.append(a)
    GB1, o = carve(ARA, o, 2048, F32)
    HB = []
    for i in range(2):
        a, o = carve(ARA, o, 2048)
        HB.append(a)
    X1 = ARA[:, :].bitcast(F32).rearrange("p (m d) -> p m d", m=NT)

    o = 0
    KT_f, o = carve(ARB, o, 4 * TPC)
    KT = KT_f.rearrange("p (h t) -> p h t", h=4)
    VN_f, o = carve(ARB, o, NT * 512)
    VN = VN_f.rearrange("p (m d) -> p m d", m=NT)
    MB = []
    for i in range(2):
        a, o = carve(ARB, o, 512, F32)
        MB.append(a.rearrange("p (h d) -> p h d", h=4))
    COS_f, o = carve(ARB, o, NT * 64, F32)
    COS = COS_f.rearrange("p (m d) -> p m d", m=NT)
    SIN_f, o = carve(ARB, o, NT * 64, F32)
    SIN = SIN_f.rearrange("p (m d) -> p m d", m=NT)
    LNT_f, o = carve(ARB, o, 2048, F32)
    LNT = LNT_f.rearrange("p (k d) -> p k d", k=4)
    WST_f, o = carve(ARB, o, 1024)
    WST = WST_f.rearrange("p (h t) -> p h t", h=8)
    WSP_f, o = carve(ARB, o, 1024)
    WSP = WSP_f.rearrange("p (h s) -> p h s", h=8)
    TABS, o = carve(ARB, o, 32, F32)
    COEF, o = carve(ARB, o, 64, F32)
    RST_f, o = carve(ARB, o, 512, F32)
    RST = RST_f.rearrange("p (h d) -> p h d", h=4)
    RN = []
    for i in range(2):
        a, o = carve(ARB, o, 512)
        RN.append(a.rearrange("p (h d) -> p h d", h=4))
    AGL = []
    for i in range(2):
        a, o = carve(ARB, o, 2 * 512, F32)
        AGL.append(a.rearrange("p (r h d) -> p r h d", r=2, h=4))
    TT = []
    for i in range(2):
        tl = []
        for j in range(4):
            a, o = carve(ARB, o, 256, F32)
            tl.append(a.rearrange("p (h d) -> p h d", h=4))
        TT.append(tl)
    KR32_fs, KR32s = [], []
    for i in range(2):
        a, o = carve(ARB, o, 512, F32)
        KR32_fs.append(a)
        KR32s.append(a.rearrange("p (h r d) -> p h r d", h=4, r=2))
    KRB = []
    for i in range(3):
        a, o = carve(ARB, o, 512)
        KRB.append(a)
    SQ_fs, SQs, Z_fs, Zs = [], [], [], []
    for i in range(2):
        a, o = carve(ARB, o, 512, F32)
        SQ_fs.append(a)
        SQs.append(a.rearrange("p (h d) -> p h d", h=4))
        a, o = carve(ARB, o, 512, F32)
        Z_fs.append(a)
        Zs.append(a.rearrange("p (h d) -> p h d", h=4))
    STB = []
    for i in range(2):
        a, o = carve(ARB, o, 512)
        STB.append(a.rearrange("p (h d) -> p h d", h=4))
    assert o <= 39680, o
    o = 0
    WD = []
    for i in range(2):
        a, o = carve(ARB, o, 8192)
        WD.append(a.rearrange("p (c e) -> p c e", c=4))
    AT = []
    for i in range(2):
        a, o = carve(ARB, o, 4096)
        AT.append(a.rearrange("p (c t) -> p c t", c=4))
    SG = []
    for i in range(2):
        a, o = carve(ARB, o, 512)
        SG.append(a)
    HB2 = []
    for i in range(2):
        a, o = carve(ARB, o, 2048)
        HB2.append(a)
    GB2, o = carve(ARB, o, 2048, F32)
    assert o <= 39680, o

    WB = [w[:, :].rearrange("p (c e) -> p c e", c=DC) for w in WBT]
    WG = [w[:, 0:4096].rearrange("p (c e) -> p c e", c=DC) for w in WBT]
    WU = [w[:, 4096:8192].rearrange("p (c e) -> p c e", c=DC) for w in WBT]

    BANK = [es.enter_context(nc.psum_tensor(f"bank{i}", [128, 512], F32)) for i in range(8)]
    bBANK = [P.buf(f"bank{i}") for i in range(8)]

    def bank4(i):
        return BANK[i][:, :].rearrange("p (h d) -> p h d", h=4)

    def bank_bf(i):
        return BANK[i][:, :].bitcast(BF16).rearrange("p (g d) -> p g d", g=8)

    bTAB = P.buf("tab")
    bXIN = [P.buf(f"xin{i}") for i in range(2)]
    bHB = [P.buf(f"hb{i}") for i in range(2)]
    bGB = P.buf("gb")
    bHT = [P.buf(f"ht{m}") for m in range(NT)]
    bWB = [P.buf(f"wb{i}") for i in range(2)]
    bSTAT = P.buf("stat")
    bSTc = [P.buf(f"statc{i}") for i in range(24)]
    bV = [P.buf(f"v{m}") for m in range(NT)]
    bKZ = [P.buf(f"kz{m}") for m in range(NT)]
    bKT = [P.buf(f"kt{m}") for m in range(NT)]
    bQT = [P.buf(f"qt{m}") for m in range(NT)]
    bG = [P.buf(f"g{m}") for m in range(NT)]
    bVN = [P.buf(f"vn{m}") for m in range(NT)]
    bMB = [P.buf(f"mb{i}") for i in range(2)]
    bMIX = [P.buf(f"mix{m}") for m in range(NT)]
    bRST = P.buf("rst")
    bRN = [P.buf(f"rn{i}") for i in range(2)]
    bAGL = [P.buf(f"agl{i}") for i in range(2)]
    bTT = [P.buf(f"tt{i}") for i in range(2)]
    bKR32s = [P.buf(f"kr32_{i}") for i in range(2)]
    bLNT = P.buf("lnt")
    bKRB = [P.buf(f"krb{i}") for i in range(3)]
    bSQs = [P.buf(f"sq{i}") for i in range(2)]
    bZs = [P.buf(f"z{i}") for i in range(2)]
    bSTB = [P.buf(f"stb{i}") for i in range(2)]
    bST4 = [P.buf(f"st4_{i}") for i in range(8)]
    bWST = P.buf("wst")
    bWSP = P.buf("wsp")
    bIB = [P.buf(f"ib{g}") for g in range(2)]
    bOB = [P.buf(f"ob{g}") for g in range(2)]
    bX1 = [P.buf(f"x1_{m}") for m in range(NT)]
    bWD = [P.buf(f"wd{i}") for i in range(2)]
    bAT = [P.buf(f"at{i}") for i in range(2)]
    bSG = [P.buf(f"sg{i}") for i in range(2)]
    bWGU = [P.buf(f"wgu{i}") for i in range(2)]
    bHB2 = [P.buf(f"hb2_{i}") for i in range(2)]
    bGB2 = P.buf("gb2")
    phaseA_bufs = list(P.bufs)

    def tab_load(dst, src, q=None):
        P.dma(q or sp, dst, src, owner=bTAB, writes=[])
    n_tab = 0
    for dst, src in ((IDENT[:], ident_d[:, :]), (MASK[:], mask_d[:, :]), (COS_f, cos_d.rearrange("p m d -> p (m d)")),
                     (SIN_f, sin_d.rearrange("p m d -> p (m d)")), (TABS, tabs_d[:, :]),
                     (COEF, coef_d[0:1, :].broadcast_to([128, 64]))):
        tab_load(dst, src)
        n_tab += 1
    tab_load(GB1, n1g[0:1, :].broadcast_to([128, D]))
    n_tab += 1
    TABTOK = (bTAB.dsem, 16 * n_tab, "d_tab")
    P.dma(pool, WSP, wsp_d.rearrange("h t s -> t h s"), owner=bWSP, writes=[bWSP])
    P.op(pool, I("memset", STAT[:], 0.0), writes=[bSTAT])
    tNH = P.op(pool, I("memset", NHALF[:], -0.5))
    EPSC = sb("epsc", [128, 4], F32)
    tEPS = P.op(pool, I("memset", EPSC[:], EPS))
    P.pegroup([I("transpose", out=bank_bf(3)[:, h, :], in_=WSP[:, h, :], identity=IDENT[:]) for h in range(8)],
              reads=[bWSP], writes=[bBANK[3]], extra=[TABTOK])
    P.op(dve, I("tensor_tensor", out=WST, in0=bank_bf(3), in1=MASK[:].unsqueeze(1).broadcast_to([128, 8, 128]),
                                        op=ALU.mult), reads=[bBANK[3]], writes=[bWST], extra=[TABTOK])

    class _Stop(Exception):
        pass

    def chk(name):
        if stop_after == name:
            raise _Stop()

    wq = {"n": 0}

    def load_wblock(src_ap):
        slot = wq["n"] % 2
        wq["n"] += 1
        P.dma(pool, WB[slot], src_ap, owner=bWB[slot], writes=[bWB[slot]])
        return slot

    pj = {"n": 0}

    def next_pj():
        b = pj["n"] % 3
        pj["n"] += 1
        return b

    def proj_group(slot, m, bank):
        fns = [I("matmul", BANK[bank][:, :], lhsT=HT[:, c, m * 128:(m + 1) * 128], rhs=WB[slot][:, c, :],
                                       start=(c == 0), stop=(c == DC - 1)) for c in range(DC)]
        return P.pegroup(fns, reads=[bHT[m], bWB[slot]], writes=[bBANK[bank]])

    def rms_stats(src_ap, src_buf, junk_ap, junk_buf, col):
        P.op(act, I("activation", out=junk_ap, in_=src_ap, func=AF.Square, accum_out=STAT[:, col:col + 1]),
             reads=[src_buf, bSTAT], writes=[junk_buf, bSTc[col]])
        P.op(dve, I("tensor_scalar", out=RSTD[:, col:col + 1], in0=STAT[:, col:col + 1], scalar1=1.0 / D,
                                            scalar2=EPS, op0=ALU.mult, op1=ALU.add), reads=[bSTc[col]], writes=[bSTc[col]])
        P.op(pool, I("tensor_tensor", out=RSTD[:, col:col + 1], in0=RSTD[:, col:col + 1], in1=NHALF[:, 0:1],
                                             op=ALU.pow), reads=[bSTc[col]], writes=[bSTc[col]], extra=[tNH])

    def transposes_to_HT(src_ap, src_buf, m):
        for half in range(2):
            bk = 3 + half
            P.pegroup([I("transpose", out=bank_bf(bk)[:, c % 8, :], in_=src_ap[:, c * 128:(c + 1) * 128],
                                                          identity=IDENT[:]) for c in range(half * 8, half * 8 + 8)],
                      reads=[src_buf], writes=[bBANK[bk]], extra=[TABTOK])
            P.op(act, I("activation", out=HT[:, half * 8:half * 8 + 8, m * 128:(m + 1) * 128],
                                                               in_=bank_bf(bk), func=AF.Copy),
                 reads=[bBANK[bk]], writes=[bHT[m]])

    st4 = {"n": 0}

    def group_stats(bank, eps_ap):
        s = st4["n"] % 8
        zs = st4["n"] % 2
        st4["n"] += 1
        b = bST4[s]
        Z, SQ, SQ_f, bZ, bSQ = Zs[zs], SQs[zs], SQ_fs[zs], bZs[zs], bSQs[zs]
        chk("vs_a")
        P.op(pool, I("memset", ST4[:, s, :], 0.0), writes=[b])
        fns = [I("activation", out=Z[:, h, :], in_=bank4(bank)[:, h, :], func=AF.Identity,
                 accum_out=ST4[:, s, h:h + 1]) for h in range(4)]
        fns += [I("activation", out=SQ[:, h, :], in_=bank4(bank)[:, h, :], func=AF.Square,
                  accum_out=ST4[:, s, 4 + h:5 + h]) for h in range(4)]
        P.group(act, fns, reads=[bBANK[bank]], writes=[bSQ, bZ, b])
        chk("vs_c")
        P.op(dve, I("tensor_scalar", out=ST4[:, s, 0:4], in0=ST4[:, s, 0:4], scalar1=1.0 / 128, scalar2=None,
                                            op0=ALU.mult), reads=[b], writes=[b])
        P.op(dve, I("scalar_tensor_tensor", out=ST4[:, s, 4:8], in0=ST4[:, s, 4:8], scalar=1.0 / 128,
                                                   in1=eps_ap, op0=ALU.mult, op1=ALU.add), reads=[b], writes=[b],
             extra=[TABTOK])
        P.op(dve, I("tensor_tensor", out=SQ_f[:, 0:4], in0=ST4[:, s, 0:4], in1=ST4[:, s, 0:4], op=ALU.mult),
             reads=[b], writes=[bSQ])
        P.op(dve, I("tensor_tensor", out=ST4[:, s, 4:8], in0=ST4[:, s, 4:8], in1=SQ_f[:, 0:4], op=ALU.subtract),
             reads=[b, bSQ], writes=[b])
        chk("vs_d")
        P.op(pool, I("tensor_tensor", out=ST4[:, s, 4:8], in0=ST4[:, s, 4:8], in1=NHALF[:, 0:4], op=ALU.pow),
             reads=[b], writes=[b], extra=[tNH])
        chk("vs_e")
        return s, b, zs

    def rotary(bank, m, tslot):
        KR32, bKR32 = KR32s[tslot], bKR32s[tslot]
        pv = BANK[bank][:, :].rearrange("p (h r d) -> p h r d", h=4, r=2)
        x1, x2 = pv[:, :, 0, :], pv[:, :, 1, :]
        cb = COS[:, m, :].unsqueeze(1).broadcast_to([128, 4, 64])
        sbb = SIN[:, m, :].unsqueeze(1).broadcast_to([128, 4, 64])
        T = TT[tslot]
        bT = bTT[tslot]
        P.group(dve, [I("tensor_tensor", out=T[j], in0=a, in1=b_, op=ALU.mult)
                      for j, (a, b_) in enumerate(((x1, cb), (x2, sbb), (x1, sbb), (x2, cb)))],
                reads=[bBANK[bank]], writes=[bT], extra=[TABTOK])
        P.group(pool, [I("tensor_tensor", out=KR32[:, :, 0, :], in0=T[0], in1=T[1], op=ALU.subtract),
                       I("tensor_tensor", out=KR32[:, :, 1, :], in0=T[2], in1=T[3], op=ALU.add)],
                reads=[bT], writes=[bKR32])

    for m in range(NT):
        i = m % 2
        P.dma(sp, XIN[i], x[m * 128:(m + 1) * 128, :], owner=bXIN[i], writes=[bXIN[i]])
        rms_stats(XIN[i], bXIN[i], HB[i], bHB[i], m)
        P.op(dve, I("scalar_tensor_tensor", out=HB[i], in0=XIN[i], scalar=RSTD[:, m:m + 1], in1=GB1,
                                                             op0=ALU.mult, op1=ALU.mult),
             reads=[bXIN[i], bSTc[m]], writes=[bHB[i]], extra=[TABTOK])
        transposes_to_HT(HB[i], bHB[i], m)

    done = {"flag": stop_after == "norm1"}

    def mixers(grp):
        hs = [4 * grp + h for h in range(4)]
        slot = load_wblock(w_in_v[:, :, OFF_V + 512 * grp: OFF_V + 512 * grp + 512])
        for m in range(NT):
            bk = next_pj()
            proj_group(slot, m, bk)
            P.op(act, I("activation", out=V[:, m, :], in_=BANK[bk][:, :], func=AF.Copy),
                 reads=[bBANK[bk]], writes=[bV[m]])
        chk("v")
        slot = load_wblock(w_in_v[:, :, OFF_K + 512 * grp: OFF_K + 512 * grp + 512])
        P.op(pool, I("memset", RST_f, 0.0), writes=[bRST])

        def k_post(m):
            ks = m % 3
            P.pegroup([I("transpose", out=bank_bf(3)[:, h, :], in_=KRB[ks][:, h * 128:(h + 1) * 128],
                                                          identity=IDENT[:]) for h in range(4)],
                      reads=[bKRB[ks]], writes=[bBANK[3]], extra=[TABTOK])
            P.op(act, I("activation", out=KT[:, :, m * 128:(m + 1) * 128], in_=bank_bf(3)[:, 0:4, :],
                                                  func=AF.Copy), reads=[bBANK[3]], writes=[bKT[m]])
            state_update(m)

        def state_update(m):
            P.pegroup([I("matmul", bank4(5)[:, h, :], lhsT=KZ[:, m, h * 128:(h + 1) * 128],
                                                    rhs=V[:, m, h * 128:(h + 1) * 128], start=True, stop=True)
                       for h in range(4)], reads=[bKZ[m], bV[m]], writes=[bBANK[5]])
            P.group(dve, [I("scalar_tensor_tensor", out=RST[:, h, :], in0=RST[:, h, :], scalar=gamma_c[hs[h]],
                                                                 in1=bank4(5)[:, h, :], op0=ALU.mult, op1=ALU.add)
                          for h in range(4)], reads=[bBANK[5]], writes=[bRST])

        for m in range(NT):
            bk = next_pj()
            proj_group(slot, m, bk)
            rotary(bk, m, m % 2)
            ks = m % 3
            KR32_f, bKR32 = KR32_fs[m % 2], bKR32s[m % 2]
            P.op(act, I("activation", out=KRB[ks], in_=KR32_f, func=AF.Copy), reads=[bKR32], writes=[bKRB[ks]])
            P.op(pool, I("tensor_tensor", out=KZ[:, m, :].rearrange("p (h d) -> p h d", h=4),
                                                      in0=KR32_f.rearrange("p (h d) -> p h d", h=4),
                                                      in1=TABS[:, hs[0]:hs[0] + 4].unsqueeze(2).broadcast_to([128, 4, 128]),
                                                      op=ALU.mult), reads=[bKR32], writes=[bKZ[m]], extra=[TABTOK])
            if m >= 2:
                k_post(m - 2)
        k_post(NT - 2)
        k_post(NT - 1)
        chk("k")
        P.dma(sp, ib[grp].ap()[:, :], RST_f, owner=bIB[grp], reads=[bRST], writes=[bIB[grp]])
        ccsem = P.sem(f"cc{grp}")
        if not nocc:
            pool.wait(bIB[grp].wtoks() + bOB[grp].alltoks())
            pool.ops.append(lambda e, grp=grp, ccsem=ccsem: e.collective_compute(
                "AllGather", ALU.bypass, replica_groups=[list(range(NCORES))],
                ins=[ib[grp].ap().opt()], outs=[ob[grp].ap().opt()]).then_inc(ccsem))
            tcc = (ccsem, 1, f"cc{grp}")
            bIB[grp].add_read(tcc)
            bOB[grp].add_write(tcc)
        chk("ag")
        slot = load_wblock(w_in_v[:, :, OFF_VS + 512 * grp: OFF_VS + 512 * grp + 512])
        for k in range(4):
            P.dma(sp, LNT[:, k, :], lnt_d[k:k + 1, 512 * grp:512 * grp + 512].broadcast_to([128, 512]), owner=bLNT,
                  writes=[bLNT] if k == 0 else [])
        bLNT.add_write((bLNT.dsem, 16 * bLNT.dcnt, "d_" + bLNT.name))
        lng = LNT[:, 0, :].rearrange("p (h d) -> p h d", h=4)
        lnb = LNT[:, 1, :].rearrange("p (h d) -> p h d", h=4)
        gng = LNT[:, 2, :].rearrange("p (h d) -> p h d", h=4)
        gnb = LNT[:, 3, :].rearrange("p (h d) -> p h d", h=4)

        def normalize(bank, s, b, zs):
            Z, bZ = Zs[zs], bZs[zs]
            P.group(dve, [I("tensor_scalar", out=Z[:, h, :], in0=Z[:, h, :],
                                                          scalar1=ST4[:, s, h:h + 1], scalar2=ST4[:, s, 4 + h:5 + h],
                                                          op0=ALU.subtract, op1=ALU.mult) for h in range(4)],
                    reads=[b], writes=[bZ])

        for m in range(NT):
            bk = next_pj()
            proj_group(slot, m, bk)
            s, b, zs = group_stats(bk, EPSC[:])
            dve.wait([tEPS])
            normalize(bk, s, b, zs)
            chk("vs_f")
            Z, bZ = Zs[zs], bZs[zs]
            P.op(dve, I("tensor_tensor", out=Z, in0=Z, in1=lng, op=ALU.mult), reads=[bZ, bLNT], writes=[bZ])
            P.op(dve, I("tensor_tensor", out=VN[:, m, :].rearrange("p (h d) -> p h d", h=4), in0=Z, in1=lnb,
                                                     op=ALU.add), reads=[bZ, bLNT], writes=[bVN[m]])
        chk("vs")
        slot = load_wblock(w_in_v[:, :, OFF_U + 512 * grp: OFF_U + 512 * grp + 512])
        for m in range(NT):
            i = m % 2
            P.pegroup([I("matmul", bank4(6)[:, h, :], lhsT=WST[:, hs[h], :],
                                                    rhs=VN[:, m, h * 128:(h + 1) * 128], start=True, stop=True)
                       for h in range(4)], reads=[bWST, bVN[m]], writes=[bBANK[6]])
            P.group(act, [I("activation", out=MB[i][:, h, :], in_=bank4(6)[:, h, :], func=AF.Identity,
                                                            bias=TABS[:, 24 + hs[h]:25 + hs[h]], scale=1.0)
                          for h in range(4)], reads=[bBANK[6]], writes=[bMB[i]], extra=[TABTOK])
            bk = next_pj()
            proj_group(slot, m, bk)
            P.op(dve, I("tensor_tensor",
                out=MIXED[:, m, 512 * grp:512 * grp + 512], in0=BANK[bk][:, :],
                in1=MB[i].rearrange("p h d -> p (h d)"), op=ALU.mult), reads=[bBANK[bk], bMB[i]], writes=[bMIX[m]])
        chk("sgu")
        slot = load_wblock(w_in_v[:, :, OFF_G + 512 * grp: OFF_G + 512 * grp + 512])
        for m in range(NT):
            bk = next_pj()
            proj_group(slot, m, bk)
            P.op(act, I("activation", out=G[:, m, :], in_=BANK[bk][:, :], func=AF.Silu),
                 reads=[bBANK[bk]], writes=[bG[m]])
        chk("g")
        slot = load_wblock(w_in_v[:, :, OFF_Q + 512 * grp: OFF_Q + 512 * grp + 512])

        def q_post(m):
            ks = m % 3
            P.pegroup([I("transpose", out=bank_bf(4)[:, h, :], in_=KRB[ks][:, h * 128:(h + 1) * 128],
                                                          identity=IDENT[:]) for h in range(4)],
                      reads=[bKRB[ks]], writes=[bBANK[4]], extra=[TABTOK])
            P.op(act, I("activation", out=QT[:, :, m * 128:(m + 1) * 128], in_=bank_bf(4)[:, 0:4, :],
                                                  func=AF.Copy), reads=[bBANK[4]], writes=[bQT[m]])

        for m in range(NT):
            bk = next_pj()
            proj_group(slot, m, bk)
            rotary(bk, m, m % 2)
            ks = m % 3
            KR32_f, bKR32 = KR32_fs[m % 2], bKR32s[m % 2]
            P.op(act, I("activation", out=KRB[ks], in_=KR32_f, func=AF.Copy), reads=[bKR32], writes=[bKRB[ks]])
            if m >= 2:
                q_post(m - 2)
        q_post(NT - 2)
        q_post(NT - 1)
        chk("q")
        for piece in range(4):
            half = piece % 2
            if nocc:
                P.dma(sp, AGL[half][:, 0].rearrange("p h d -> p (h d)"), ib[grp].ap()[:, :],
                      owner=bAGL[half], reads=[bIB[grp]], writes=[bAGL[half]])
                P.dma(sp, AGL[half][:, 1].rearrange("p h d -> p (h d)"), ib[grp].ap()[:, :],
                      owner=bAGL[half], reads=[bIB[grp]], writes=[bAGL[half]])
            else:
                P.dma(sp, AGL[half].rearrange("p r h d -> p r (h d)"),
                      ob[grp].ap()[piece * 256:(piece + 1) * 256, :].rearrange("(r p) f -> p r f", p=128),
                      owner=bAGL[half], reads=[bOB[grp]], writes=[bAGL[half]])
            for r in range(2):
                cp = piece * 2 + r
                if cp == 0:
                    fns = [I("tensor_scalar",
                        out=RST[:, h, :], in0=AGL[half][:, r, h, :], scalar1=COEF[:, hs[h]:hs[h] + 1], scalar2=None,
                        op0=ALU.mult) for h in range(4)]
                else:
                    fns = [I("scalar_tensor_tensor",
                        out=RST[:, h, :], in0=AGL[half][:, r, h, :], scalar=COEF[:, cp * 8 + hs[h]:cp * 8 + hs[h] + 1],
                        in1=RST[:, h, :], op0=ALU.mult, op1=ALU.add) for h in range(4)]
                P.group(dve, fns, reads=[bAGL[half]], writes=[bRST], extra=[TABTOK])
        chk("comb")
        for n in range(NT):
            i = n % 2
            P.op(act, I("activation", out=RN[i].rearrange("p h d -> p (h d)"), in_=RST_f, func=AF.Copy),
                 reads=[bRST], writes=[bRN[i]])
            P.pegroup([I("matmul", bank4(6)[:, h, :], lhsT=KT[:, h, n * 128:(n + 1) * 128],
                                                    rhs=QT[:, h, n * 128:(n + 1) * 128], start=True, stop=True)
                       for h in range(4)], reads=[bKT[n], bQT[n]], writes=[bBANK[6]])
            P.group(dve, [I("scalar_tensor_tensor",
                out=STB[i][:, h, :], in0=bank4(6)[:, h, :], scalar=TABS[:, 8 + hs[h]:9 + hs[h]], in1=MASK[:],
                op0=ALU.mult, op1=ALU.mult) for h in range(4)], reads=[bBANK[6]], writes=[bSTB[i]], extra=[TABTOK])
            fns = []
            for h in range(4):
                fns.append(I("matmul", bank4(7)[:, h, :], lhsT=STB[i][:, h, :],
                                                             rhs=V[:, n, h * 128:(h + 1) * 128], start=True, stop=False))
                fns.append(I("matmul", bank4(7)[:, h, :], lhsT=QT[:, h, n * 128:(n + 1) * 128],
                                                             rhs=RN[i][:, h, :], start=False, stop=True))
            P.pegroup(fns, reads=[bSTB[i], bV[n], bQT[n], bRN[i]], writes=[bBANK[7]])
            s, b, zs = group_stats(7, TABS[:, 16 + hs[0]:16 + hs[0] + 4])
            normalize(7, s, b, zs)
            Z, Z_f, bZ = Zs[zs], Z_fs[zs], bZs[zs]
            P.op(dve, I("tensor_tensor", out=Z, in0=Z, in1=gng, op=ALU.mult), reads=[bZ, bLNT], writes=[bZ])
            P.op(dve, I("tensor_tensor", out=Z, in0=Z, in1=gnb, op=ALU.add), reads=[bZ, bLNT], writes=[bZ])
            P.op(dve, I("tensor_tensor",
                out=MIXED[:, n, 1024 + 512 * grp:1024 + 512 * grp + 512], in0=Z_f, in1=G[:, n, :], op=ALU.mult),
                reads=[bZ, bG[n]], writes=[bMIX[n]])
            if n < NT - 1:
                state_update(n)

    try:
        for grp in range((1 if stop_after == "mix0" else 2) if not done["flag"] else 0):
            mixers(grp)
    except _Stop:
        done["flag"] = True

    if stop_after in ("mix", "mix0"):
        done["flag"] = True

    if not done["flag"]:
        for m in range(NT):
            transposes_to_HT(MIXED[:, m, :], bMIX[m], m)
        phaseA_toks = []
        for b in phaseA_bufs:
            phaseA_toks += b.alltoks()
        for m in range(NT):
            P.dma(sp, X1[:, m, :], x[m * 128:(m + 1) * 128, :], owner=bX1[m], writes=[bX1[m]], extra=phaseA_toks)
        P.dma(sp, GB2, n2g[0:1, :].broadcast_to([128, D]), owner=bGB2, writes=[bGB2], extra=phaseA_toks)
        for cb in range(4):
            slot = load_wblock(w_out_v[:, :, cb * 512:(cb + 1) * 512])
            for m in range(NT):
                bk = next_pj()
                proj_group(slot, m, bk)
                P.op(dve, I("tensor_tensor",
                    out=X1[:, m, cb * 512:(cb + 1) * 512], in0=BANK[bk][:, :], in1=X1[:, m, cb * 512:(cb + 1) * 512],
                    op=ALU.add), reads=[bBANK[bk], bX1[m]], writes=[bX1[m]])
    if stop_after == "oproj":
        done["flag"] = True

    if not done["flag"]:
        for eng_ in (act, dve, pool):
            eng_.wait(phaseA_toks)
        for m in range(NT):
            i = m % 2
            rms_stats(X1[:, m, :], bX1[m], HB2[i], bHB2[i], 8 + m)
            P.op(dve, I("scalar_tensor_tensor", out=HB2[i], in0=X1[:, m, :], scalar=RSTD[:, 8 + m:9 + m],
                                                                 in1=GB2, op0=ALU.mult, op1=ALU.mult),
                 reads=[bX1[m], bSTc[8 + m], bGB2], writes=[bHB2[i]])
            transposes_to_HT(HB2[i], bHB2[i], m)
        ffn_first = list(phaseA_toks)
        for b in bWB:
            ffn_first += b.alltoks()
        gq = {"n": 0}

        def load_gu(fb, sbk):
            q = gq["n"]
            gq["n"] += 1
            slot = q % 2
            f0 = fb * 512 + sbk * 256
            P.dma(pool, WG[slot], w_gate_v[:, :, f0:f0 + 256], owner=bWGU[slot], writes=[bWGU[slot]], extra=ffn_first)
            P.dma(pool, WU[slot], w_up_v[:, :, f0:f0 + 256], owner=bWGU[slot], writes=[], extra=ffn_first)
            t = (bWGU[slot].dsem, 16 * bWGU[slot].dcnt, "d_" + bWGU[slot].name)
            bWGU[slot].add_write(t)
            return slot

        def load_wd(fb):
            slot = fb % 2
            P.dma(pool, WD[slot], w_down_v[:, 4 * fb:4 * fb + 4, :], owner=bWD[slot], writes=[bWD[slot]], extra=ffn_first)

        gu_pair = {"n": 0}

        def gu_units(fb):
            for sbk in range(2):
                wslot = load_gu(fb, sbk)
                for j in range(2):
                    fc = sbk * 2 + j
                    for hh in range(2):
                        pr = gu_pair["n"] % 2
                        gu_pair["n"] += 1
                        bg, bu = 2 * pr, 2 * pr + 1
                        fns = [I("matmul",
                            BANK[bg][:, :], lhsT=WG[wslot][:, c, j * 128:(j + 1) * 128],
                            rhs=HT[:, c, hh * 512:(hh + 1) * 512], start=(c == 0), stop=(c == DC - 1)) for c in range(DC)]
                        fns += [I("matmul",
                            BANK[bu][:, :], lhsT=WU[wslot][:, c, j * 128:(j + 1) * 128],
                            rhs=HT[:, c, hh * 512:(hh + 1) * 512], start=(c == 0), stop=(c == DC - 1)) for c in range(DC)]
                        P.pegroup(fns, reads=[bWGU[wslot]] + [bHT[4 * hh + k] for k in range(4)],
                                  writes=[bBANK[bg], bBANK[bu]])
                        si = gu_pair["n"] % 2
                        P.op(act, I("activation", out=SG[si], in_=BANK[bg][:, :], func=AF.Silu),
                             reads=[bBANK[bg]], writes=[bSG[si]], extra=phaseA_toks)
                        first = (fc == 0 and hh == 0)
                        t = P.op(dve, I("tensor_tensor",
                            out=AT[fb % 2][:, fc, hh * 512:(hh + 1) * 512], in0=BANK[bu][:, :], in1=SG[si], op=ALU.mult),
                            reads=[bBANK[bu], bSG[si]], writes=[bAT[fb % 2]] if first else [])
                        if not first:
                            bAT[fb % 2].add_write(t)

        dn_bank = {"n": 0}

        def dn_units(fb):
            a = fb % 2
            for m in range(NT):
                for cb in range(4):
                    bk = 4 + dn_bank["n"] % 4
                    dn_bank["n"] += 1
                    fns = [I("matmul",
                        BANK[bk][:, :], lhsT=AT[a][:, fc, m * 128:(m + 1) * 128], rhs=WD[a][:, fc, cb * 512:(cb + 1) * 512],
                        start=(fc == 0), stop=(fc == 3)) for fc in range(4)]
                    P.pegroup(fns, reads=[bAT[a], bWD[a]], writes=[bBANK[bk]])
                    P.op(dve, I("tensor_tensor",
                        out=X1[:, m, cb * 512:(cb + 1) * 512], in0=BANK[bk][:, :], in1=X1[:, m, cb * 512:(cb + 1) * 512],
                        op=ALU.add), reads=[bBANK[bk], bX1[m]], writes=[bX1[m]])

        nfb = NFB if stop_after != "ffn1" else 1
        load_wd(0)
        gu_units(0)
        for fb in range(1, nfb):
            load_wd(fb)
            gu_units(fb)
            dn_units(fb - 1)
        dn_units(nfb - 1)
        P.dma(sp, GB2, nfg[0:1, :].broadcast_to([128, D]), owner=bGB2, writes=[bGB2])
        for m in range(NT):
            i = m % 2
            rms_stats(X1[:, m, :], bX1[m], HB2[i], bHB2[i], 16 + m)
            P.op(dve, I("scalar_tensor_tensor", out=X1[:, m, :], in0=X1[:, m, :], scalar=RSTD[:, 16 + m:17 + m],
                                                            in1=GB2, op0=ALU.mult, op1=ALU.mult),
                 reads=[bX1[m], bSTc[16 + m], bGB2], writes=[bX1[m]])
            P.dma(sp, out[m * 128:(m + 1) * 128, :], X1[:, m, :], owner=bX1[m], reads=[bX1[m]])

    for name, getter, shape, dt in dumps:
        dten = nc.dram_tensor("dump_" + name, shape, dt, kind="ExternalOutput").ap()
        ap = getter(locals())
        bd = P.buf("dump_" + name)
        alltoks = []
        for b in P.bufs:
            alltoks += b.alltoks()
        P.dma(sp, dten, ap, owner=bd, extra=alltoks, reads=[bd])

    final = []
    for b in P.bufs:
        final += b.alltoks()
    sp.wait(final)

    with nc.Block() as block:
        for ename, q in (("sync", sp), ("scalar", act), ("vector", dve), ("gpsimd", pool), ("tensor", pe)):
            def body(e, q=q):
                for fn in q.ops:
                    fn(e)
            getattr(block, ename)(body)
    es.close()
    return nc


def make_in_maps(x, norm1_g, w_in, sgu_ln_g, sgu_ln_b, w_spatial, b_spatial, ret_gn_g, ret_gn_b,
                 w_out, norm2_g, w_gate, w_up, w_down, final_norm_g):
    f32 = lambda a: np.ascontiguousarray(np.asarray(a), dtype=np.float32)
    zeta, gneg, eps2, _, mask, ident = _const_tables()
    tabs = np.concatenate([zeta, gneg, eps2, f32(b_spatial)[0].T.astype(np.float64)], axis=1).astype(np.float32)
    shared = {
        "w_in": f32(w_in)[0], "w_out": f32(w_out)[0], "w_gate": f32(w_gate)[0], "w_up": f32(w_up)[0],
        "w_down": f32(w_down)[0],
        "n1g": f32(norm1_g).reshape(1, D), "n2g": f32(norm2_g).reshape(1, D), "nfg": f32(final_norm_g).reshape(1, D),
        "lnt": np.stack([f32(sgu_ln_g).reshape(-1), f32(sgu_ln_b).reshape(-1), f32(ret_gn_g).reshape(-1),
                         f32(ret_gn_b).reshape(-1)], axis=0),
        "wsp": f32(w_spatial)[0],
        "tabs": np.ascontiguousarray(tabs),
        "ident": ident.astype(ml_dtypes.bfloat16), "mask": mask.astype(ml_dtypes.bfloat16),
    }
    xf = f32(x)[0]
    in_maps = []
    for c in range(NCORES):
        cos, sin, coef = _core_tables(c)
        m = dict(shared)
        m["x"] = np.ascontiguousarray(xf[c * TPC:(c + 1) * TPC])
        m["cos"], m["sin"], m["coef"] = cos, sin, coef
        in_maps.append(m)
    return in_maps


_NC_CACHE = {}


def kernel(**inputs):
    in_maps = make_in_maps(**inputs)
    if "nc" not in _NC_CACHE:
        _NC_CACHE["nc"] = build_program()
    res = run_bass_kernel_spmd(_NC_CACHE["nc"], in_maps, core_ids=list(range(NCORES)))
    return np.concatenate([r["out"] for r in res.results], axis=0).reshape(1, SEQ, D).astype(np.float32)
```

```python
import contextlib
import numpy as np
import ml_dtypes
import concourse.bass as bass
import concourse.mybir as mybir
from concourse.bass_utils import run_bass_kernel_spmd

F32 = mybir.dt.float32
BF16 = mybir.dt.bfloat16
AF = mybir.ActivationFunctionType
ALU = mybir.AluOpType
AX = mybir.AxisListType

NCORES = 8
SEQ = 8192
D = 2048
TPC = SEQ // NCORES
NT = TPC // 128
DC = D // 128
INW = 6144
DFF = 5632
NFB = DFF // 512
EPS = 1e-6
OFF_U, OFF_VS, OFF_Q, OFF_K, OFF_V, OFF_G = 0, 1024, 2048, 3072, 4096, 5120


def I(name, *args, **kw):
    return lambda e: getattr(e, name)(*args, **kw)


class Eng:
    def __init__(self, name, sem):
        self.name = name
        self.sem = sem
        self.ops = []
        self.cnt = 0
        self.seen = {}

    def wait(self, toks):
        best = {}
        for t in toks:
            if t is None:
                continue
            if t[2] not in best or best[t[2]][1] < t[1]:
                best[t[2]] = t
        for sem, val, key in best.values():
            if self.seen.get(key, 0) >= val:
                continue
            self.seen[key] = val
            self.ops.append(I("wait_ge", sem, val))

    def emit(self, fn, sig=True):
        if sig:
            self.cnt += 1
            c = self.cnt
            sem = self.sem
            self.ops.append(lambda e, fn=fn, sem=sem: fn(e).then_inc(sem, 1))
            return (self.sem, c, self.name)
        self.ops.append(lambda e, fn=fn: fn(e))
        return None


class Buf:
    def __init__(self, name):
        self.name = name
        self.w = {}
        self.r = {}
        self.dsem = None
        self.dcnt = 0

    def wtoks(self):
        return list(self.w.values())

    def rtoks(self):
        return list(self.r.values())

    def alltoks(self):
        return self.wtoks() + self.rtoks()

    def add_read(self, t):
        k = t[2]
        if k not in self.r or self.r[k][1] < t[1]:
            self.r[k] = t

    def add_write(self, t):
        self.w = {t[2]: t}
        self.r = {}


class Prog:
    def __init__(self, nc, es):
        self.nc = nc
        self.es = es
        self.nsem = 0
        self.act = Eng("act", self.sem("s_act"))
        self.dve = Eng("dve", self.sem("s_dve"))
        self.pool = Eng("pool", self.sem("s_pool"))
        self.pe = Eng("pe", self.sem("s_pe"))
        self.sp = Eng("sp", self.sem("s_sp"))
        self.bufs = []

    def sem(self, name):
        self.nsem += 1
        return self.es.enter_context(self.nc.semaphore(name))

    def buf(self, name):
        b = Buf(name)
        self.bufs.append(b)
        return b

    def _deps(self, reads, writes, extra):
        toks = list(extra)
        for b in reads:
            toks += b.wtoks()
        for b in writes:
            toks += b.wtoks() + b.rtoks()
        return toks

    def op(self, eng, fn, reads=(), writes=(), extra=()):
        eng.wait(self._deps(reads, writes, extra))
        t = eng.emit(fn, sig=True)
        for b in reads:
            b.add_read(t)
        for b in writes:
            b.add_write(t)
        return t

    def pegroup(self, fns, reads=(), writes=(), extra=()):
        return self.group(self.pe, fns, reads, writes, extra)

    def group(self, eng, fns, reads=(), writes=(), extra=()):
        eng.wait(self._deps(reads, writes, extra))
        for fn in fns[:-1]:
            eng.emit(fn, sig=False)
        t = eng.emit(fns[-1], sig=True)
        for b in reads:
            b.add_read(t)
        for b in writes:
            b.add_write(t)
        return t

    def dma(self, q, out, in_, owner, reads=(), writes=(), extra=()):
        if owner.dsem is None:
            owner.dsem = self.sem("d_" + owner.name)
        q.wait(self._deps(reads, writes, extra))
        owner.dcnt += 1
        sem, val = owner.dsem, 16 * owner.dcnt
        q.ops.append(lambda e, out=out, in_=in_, sem=sem: e.dma_start(out=out, in_=in_).then_inc(sem, 16))
        t = (sem, val, "d_" + owner.name)
        for b in reads:
            b.add_read(t)
        for b in writes:
            b.add_write(t)
        return t


def _log_gamma():
    h = np.arange(8, dtype=np.float64)
    return np.log(1.0 - np.exp2(-5.0 - h))


def _const_tables():
    lg = _log_gamma()
    t = np.arange(128, dtype=np.float64)
    s = 128.0 ** -0.5
    zeta = s * np.exp((128.0 - t)[:, None] * lg[None, :])
    gneg = s * np.exp(-t[:, None] * lg[None, :])
    eps2 = EPS * np.exp(-2.0 * t[:, None] * lg[None, :])
    gamma_c = np.exp(128.0 * lg)
    mask = (t[:, None] <= t[None, :]).astype(np.float32)
    ident = np.eye(128, dtype=np.float32)
    return zeta, gneg, eps2, gamma_c, mask, ident


def _core_tables(core):
    lg = _log_gamma()
    half = 64
    inv = 1.0 / (10000.0 ** (np.arange(half, dtype=np.float64) / half))
    pos = core * TPC + np.arange(TPC, dtype=np.float64)
    ang = pos[:, None] * inv[None, :]
    cos = np.cos(ang).reshape(NT, 128, half).transpose(1, 0, 2)
    sin = np.sin(ang).reshape(NT, 128, half).transpose(1, 0, 2)
    coef = np.zeros((8, 8), dtype=np.float64)
    for cp in range(core):
        coef[cp, :] = np.exp(1024.0 * (core - 1 - cp) * lg)
    return (np.ascontiguousarray(cos, dtype=np.float32), np.ascontiguousarray(sin, dtype=np.float32),
            coef.reshape(1, 64).astype(np.float32))


def build_program(stop_after=None, dumps=(), nocc=False):
    nc = bass.Bass("TRN2", target_bir_lowering=False)
    es = contextlib.ExitStack()
    _, _, _, gamma_c, _, _ = _const_tables()
    gamma_c = [float(np.float32(v)) for v in gamma_c]

    def din(name, shape, dt=F32):
        return nc.dram_tensor(name, shape, dt, kind="ExternalInput").ap()

    x = din("x", [TPC, D])
    w_in = din("w_in", [D, INW])
    w_out = din("w_out", [D, D])
    lite = stop_after is not None and stop_after not in ("ffn1",)
    w_gate = din("w_gate", [D, DFF] if not lite else [128, 128])
    w_up = din("w_up", [D, DFF] if not lite else [128, 128])
    w_down = din("w_down", [DFF, D] if not lite else [128, 128])
    n1g = din("n1g", [1, D])
    n2g = din("n2g", [1, D])
    nfg = din("nfg", [1, D])
    lnt_d = din("lnt", [4, 1024])
    wsp_d = din("wsp", [8, 128, 128])
    tabs_d = din("tabs", [128, 32])
    cos_d = din("cos", [128, NT, 64])
    sin_d = din("sin", [128, NT, 64])
    coef_d = din("coef", [1, 64])
    ident_d = din("ident", [128, 128], BF16)
    mask_d = din("mask", [128, 128], BF16)
    out = nc.dram_tensor("out", [TPC, D], F32, kind="ExternalOutput").ap()
    ib = [nc.dram_tensor(f"ib{g}", [128, 512], F32) for g in range(2)]
    ob = [nc.dram_tensor(f"ob{g}", [NCORES * 128, 512], F32) for g in range(2)]
    dump_out = {}

    w_in_v = w_in.rearrange("(c p) e -> p c e", p=128)
    w_out_v = w_out.rearrange("(c p) e -> p c e", p=128)
    w_gate_v = w_gate.rearrange("(c p) f -> p c f", p=128)
    w_up_v = w_up.rearrange("(c p) f -> p c f", p=128)
    w_down_v = w_down.rearrange("(c p) e -> p c e", p=128)

    def sb(name, shape, dt):
        return es.enter_context(nc.sbuf_tensor("s_" + name, shape, dt))

    P = Prog(nc, es)
    act, dve, pool, pe, sp = P.act, P.dve, P.pool, P.pe, P.sp

    ARA = sb("arena_a", [128, 32768], BF16)
    HT = sb("ht", [128, DC, TPC], BF16)
    WBT = [sb(f"wb{i}", [128, 8192], BF16) for i in range(2)]
    ARB = sb("arena_b", [128, 39680], BF16)
    IDENT = sb("ident", [128, 128], BF16)
    MASK = sb("maskt", [128, 128], BF16)
    STAT = sb("stat", [128, 24], F32)
    RSTD = sb("rstd", [128, 24], F32)
    NHALF = sb("nhalf", [128, 8], F32)
    ST4 = sb("st4", [128, 8, 8], F32)

    def carve(arena, off, n, dt=BF16):
        if dt == BF16:
            return arena[:, off:off + n], off + n
        assert dt == F32
        return arena[:, off:off + 2 * n].bitcast(F32), off + 2 * n

    o = 0
    MIXED_f, o = carve(ARA, o, NT * 2048)
    MIXED = MIXED_f.rearrange("p (m d) -> p m d", m=NT)
    KZ_f, o = carve(ARA, o, NT * 512)
    KZ = KZ_f.rearrange("p (m d) -> p m d", m=NT)
    QT_f, o = carve(ARA, o, 4 * TPC)
    QT = QT_f.rearrange("p (h t) -> p h t", h=4)
    G_f, o = carve(ARA, o, NT * 512)
    G = G_f.rearrange("p (m d) -> p m d", m=NT)
    V_f, o = carve(ARA, o, NT * 512)
    V = V_f.rearrange("p (m d) -> p m d", m=NT)
    assert o == 32768
    o = 0
    XIN = []
    for i in range(2):
        a, o = carve(ARA, o, 2048, F32)
        XIN.append(a)
    GB1, o = carve(ARA, o, 2048, F32)
    HB = []
    for i in range(2):
        a, o = carve(ARA, o, 2048)
        HB.append(a)
    o = 16384
    for i in range(2):
        a, o = carve(ARA, o, 2048, F32)
        XIN.append(a)
    for i in range(2):
        a, o = carve(ARA, o, 2048)
        HB.append(a)
    X1 = ARA[:, :].bitcast(F32).rearrange("p (m d) -> p m d", m=NT)

    o = 0
    KT_f, o = carve(ARB, o, 4 * TPC)
    KT = KT_f.rearrange("p (h t) -> p h t", h=4)
    VN_f, o = carve(ARB, o, NT * 512)
    VN = VN_f.rearrange("p (m d) -> p m d", m=NT)
    MB = []
    for i in range(2):
        a, o = carve(ARB, o, 512, F32)
        MB.append(a.rearrange("p (h d) -> p h d", h=4))
    COS_f, o = carve(ARB, o, NT * 64, F32)
    COS = COS_f.rearrange("p (m d) -> p m d", m=NT)
    SIN_f, o = carve(ARB, o, NT * 64, F32)
    SIN = SIN_f.rearrange("p (m d) -> p m d", m=NT)
    LNT_f, o = carve(ARB, o, 2048, F32)
    LNT = LNT_f.rearrange("p (k d) -> p k d", k=4)
    WST_f, o = carve(ARB, o, 1024)
    WST = WST_f.rearrange("p (h t) -> p h t", h=8)
    WSP_f, o = carve(ARB, o, 1024)
    WSP = WSP_f.rearrange("p (h s) -> p h s", h=8)
    TABS, o = carve(ARB, o, 32, F32)
    COEF, o = carve(ARB, o, 64, F32)
    RST_f, o = carve(ARB, o, 512, F32)
    RST = RST_f.rearrange("p (h d) -> p h d", h=4)
    RN = []
    for i in range(2):
        a, o = carve(ARB, o, 512)
        RN.append(a.rearrange("p (h d) -> p h d", h=4))
    AGL = []
    for i in range(2):
        a, o = carve(ARB, o, 2 * 512, F32)
        AGL.append(a.rearrange("p (r h d) -> p r h d", r=2, h=4))
    TT = []
    for i in range(2):
        tl = []
        for j in range(4):
            a, o = carve(ARB, o, 256, F32)
            tl.append(a.rearrange("p (h d) -> p h d", h=4))
        TT.append(tl)
    KR32_fs, KR32s = [], []
    for i in range(2):
        a, o = carve(ARB, o, 512, F32)
        KR32_fs.append(a)
        KR32s.append(a.rearrange("p (h r d) -> p h r d", h=4, r=2))
    KRB = []
    for i in range(3):
        a, o = carve(ARB, o, 512)
        KRB.append(a)
    SQ_fs, SQs, Z_fs, Zs = [], [], [], []
    for i in range(2):
        a, o = carve(ARB, o, 512, F32)
        SQ_fs.append(a)
        SQs.append(a.rearrange("p (h d) -> p h d", h=4))
        a, o = carve(ARB, o, 512, F32)
        Z_fs.append(a)
        Zs.append(a.rearrange("p (h d) -> p h d", h=4))
    STB = []
    for i in range(2):
        a, o = carve(ARB, o, 512)
        STB.append(a.rearrange("p (h d) -> p h d", h=4))
    assert o <= 39680, o
    o = 0
    WD = []
    for i in range(2):
        a, o = carve(ARB, o, 8192)
        WD.append(a.rearrange("p (c e) -> p c e", c=4))
    AT = []
    for i in range(2):
        a, o = carve(ARB, o, 4096)
        AT.append(a.rearrange("p (c t) -> p c t", c=4))
    SG = []
    for i in range(2):
        a, o = carve(ARB, o, 512)
        SG.append(a)
    HB2 = []
    for i in range(4):
        a, o = carve(ARB, o, 2048)
        HB2.append(a)
    GB2, o = carve(ARB, o, 2048, F32)
    assert o <= 39680, o

    WB = [w[:, :].rearrange("p (c e) -> p c e", c=DC) for w in WBT]
    WG = [w[:, 0:4096].rearrange("p (c e) -> p c e", c=DC) for w in WBT]
    WU = [w[:, 4096:8192].rearrange("p (c e) -> p c e", c=DC) for w in WBT]

    BANK = [es.enter_context(nc.psum_tensor(f"bank{i}", [128, 512], F32)) for i in range(8)]
    bBANK = [P.buf(f"bank{i}") for i in range(8)]

    def bank4(i):
        return BANK[i][:, :].rearrange("p (h d) -> p h d", h=4)

    def bank_bf(i):
        return BANK[i][:, :].bitcast(BF16).rearrange("p (g d) -> p g d", g=8)

    bTAB = P.buf("tab")
    bXIN = [P.buf(f"xin{i}") for i in range(4)]
    bHB = [P.buf(f"hb{i}") for i in range(4)]
    bGB = P.buf("gb")
    bHT = [P.buf(f"ht{m}") for m in range(NT)]
    bWB = [P.buf(f"wb{i}") for i in range(2)]
    bSTAT = P.buf("stat")
    bSTc = [P.buf(f"statc{i}") for i in range(24)]
    bV = [P.buf(f"v{m}") for m in range(NT)]
    bKZ = [P.buf(f"kz{m}") for m in range(NT)]
    bKT = [P.buf(f"kt{m}") for m in range(NT)]
    bQT = [P.buf(f"qt{m}") for m in range(NT)]
    bG = [P.buf(f"g{m}") for m in range(NT)]
    bVN = [P.buf(f"vn{m}") for m in range(NT)]
    bMB = [P.buf(f"mb{i}") for i in range(2)]
    bMIX = [P.buf(f"mix{m}") for m in range(NT)]
    bRST = P.buf("rst")
    bRN = [P.buf(f"rn{i}") for i in range(2)]
    bAGL = [P.buf(f"agl{i}") for i in range(2)]
    bTT = [P.buf(f"tt{i}") for i in range(2)]
    bKR32s = [P.buf(f"kr32_{i}") for i in range(2)]
    bLNT = P.buf("lnt")
    bKRB = [P.buf(f"krb{i}") for i in range(3)]
    bSQs = [P.buf(f"sq{i}") for i in range(2)]
    bZs = [P.buf(f"z{i}") for i in range(2)]
    bSTB = [P.buf(f"stb{i}") for i in range(2)]
    bST4 = [P.buf(f"st4_{i}") for i in range(8)]
    bWST = P.buf("wst")
    bWSP = P.buf("wsp")
    bIB = [P.buf(f"ib{g}") for g in range(2)]
    bOB = [P.buf(f"ob{g}") for g in range(2)]
    bX1 = [P.buf(f"x1_{m}") for m in range(NT)]
    bWD = [P.buf(f"wd{i}") for i in range(2)]
    bAT = [P.buf(f"at{i}") for i in range(2)]
    bSG = [P.buf(f"sg{i}") for i in range(2)]
    bWGU = [P.buf(f"wgu{i}") for i in range(2)]
    bHB2 = [P.buf(f"hb2_{i}") for i in range(4)]
    bGB2 = P.buf("gb2")
    phaseA_bufs = list(P.bufs)

    def tab_load(dst, src, q=None):
        P.dma(q or sp, dst, src, owner=bTAB, writes=[])
    n_tab = 0
    for dst, src in ((IDENT[:], ident_d[:, :]), (MASK[:], mask_d[:, :]), (COS_f, cos_d.rearrange("p m d -> p (m d)")),
                     (SIN_f, sin_d.rearrange("p m d -> p (m d)")), (TABS, tabs_d[:, :]),
                     (COEF, coef_d[0:1, :].broadcast_to([128, 64]))):
        tab_load(dst, src)
        n_tab += 1
    tab_load(GB1, n1g[0:1, :].broadcast_to([128, D]))
    n_tab += 1
    TABTOK = (bTAB.dsem, 16 * n_tab, "d_tab")
    P.dma(pool, WSP, wsp_d.rearrange("h t s -> t h s"), owner=bWSP, writes=[bWSP])
    P.op(pool, I("memset", STAT[:], 0.0), writes=[bSTAT])
    tNH = P.op(pool, I("memset", NHALF[:], -0.5))
    EPSC = sb("epsc", [128, 4], F32)
    tEPS = P.op(pool, I("memset", EPSC[:], EPS))
    P.pegroup([I("transpose", out=bank_bf(3)[:, h, :], in_=WSP[:, h, :], identity=IDENT[:]) for h in range(8)],
              reads=[bWSP], writes=[bBANK[3]], extra=[TABTOK])
    P.op(dve, I("tensor_tensor", out=WST, in0=bank_bf(3), in1=MASK[:].unsqueeze(1).broadcast_to([128, 8, 128]),
                                        op=ALU.mult), reads=[bBANK[3]], writes=[bWST], extra=[TABTOK])

    class _Stop(Exception):
        pass

    def chk(name):
        if stop_after == name:
            raise _Stop()

    wsrc = []
    for g_ in range(2):
        for off in (OFF_V, OFF_K, OFF_VS, OFF_U, OFF_G, OFF_Q):
            wsrc.append(w_in_v[:, :, off + 512 * g_: off + 512 * g_ + 512])
    for cb_ in range(4):
        wsrc.append(w_out_v[:, :, cb_ * 512:(cb_ + 1) * 512])
    wq = {"issued": 0, "cur": 0}

    def issue_next():
        i = wq["issued"]
        if i < len(wsrc):
            P.dma(pool, WB[i % 2], wsrc[i], owner=bWB[i % 2], writes=[bWB[i % 2]])
            wq["issued"] += 1

    def load_wblock(src_ap=None):
        i = wq["cur"]
        wq["cur"] += 1
        while wq["issued"] <= i + 1 and wq["issued"] < len(wsrc):
            issue_next()
        return i % 2

    pj = {"n": 0}

    def next_pj():
        b = pj["n"] % 3
        pj["n"] += 1
        return b

    def proj_group(slot, m, bank):
        fns = [I("matmul", BANK[bank][:, :], lhsT=HT[:, c, m * 128:(m + 1) * 128], rhs=WB[slot][:, c, :],
                                       start=(c == 0), stop=(c == DC - 1)) for c in range(DC)]
        return P.pegroup(fns, reads=[bHT[m], bWB[slot]], writes=[bBANK[bank]])

    def rms_stats(src_ap, src_buf, junk_ap, junk_buf, col):
        P.op(act, I("activation", out=junk_ap, in_=src_ap, func=AF.Square, accum_out=STAT[:, col:col + 1]),
             reads=[src_buf, bSTAT], writes=[junk_buf, bSTc[col]])
        P.op(dve, I("tensor_scalar", out=RSTD[:, col:col + 1], in0=STAT[:, col:col + 1], scalar1=1.0 / D,
                                            scalar2=EPS, op0=ALU.mult, op1=ALU.add), reads=[bSTc[col]], writes=[bSTc[col]])
        P.op(pool, I("tensor_tensor", out=RSTD[:, col:col + 1], in0=RSTD[:, col:col + 1], in1=NHALF[:, 0:1],
                                             op=ALU.pow), reads=[bSTc[col]], writes=[bSTc[col]], extra=[tNH])

    def transposes_to_HT(src_ap, src_buf, m):
        for half in range(2):
            bk = 3 + half
            P.pegroup([I("transpose", out=bank_bf(bk)[:, c % 8, :], in_=src_ap[:, c * 128:(c + 1) * 128],
                                                          identity=IDENT[:]) for c in range(half * 8, half * 8 + 8)],
                      reads=[src_buf], writes=[bBANK[bk]], extra=[TABTOK])
            P.op(act, I("activation", out=HT[:, half * 8:half * 8 + 8, m * 128:(m + 1) * 128],
                                                               in_=bank_bf(bk), func=AF.Copy),
                 reads=[bBANK[bk]], writes=[bHT[m]])

    st4 = {"n": 0}

    def group_stats(bank, eps_ap):
        s = st4["n"] % 8
        zs = st4["n"] % 2
        st4["n"] += 1
        b = bST4[s]
        Z, SQ, SQ_f, bZ, bSQ = Zs[zs], SQs[zs], SQ_fs[zs], bZs[zs], bSQs[zs]
        chk("vs_a")
        P.op(pool, I("memset", ST4[:, s, :], 0.0), writes=[b])
        fns = [I("activation", out=Z[:, h, :], in_=bank4(bank)[:, h, :], func=AF.Identity,
                 accum_out=ST4[:, s, h:h + 1]) for h in range(4)]
        fns += [I("activation", out=SQ[:, h, :], in_=bank4(bank)[:, h, :], func=AF.Square,
                  accum_out=ST4[:, s, 4 + h:5 + h]) for h in range(4)]
        P.group(act, fns, reads=[bBANK[bank]], writes=[bSQ, bZ, b])
        chk("vs_c")
        P.op(dve, I("tensor_scalar", out=ST4[:, s, 0:4], in0=ST4[:, s, 0:4], scalar1=1.0 / 128, scalar2=None,
                                            op0=ALU.mult), reads=[b], writes=[b])
        P.op(dve, I("scalar_tensor_tensor", out=ST4[:, s, 4:8], in0=ST4[:, s, 4:8], scalar=1.0 / 128,
                                                   in1=eps_ap, op0=ALU.mult, op1=ALU.add), reads=[b], writes=[b],
             extra=[TABTOK])
        P.op(dve, I("tensor_tensor", out=SQ_f[:, 0:4], in0=ST4[:, s, 0:4], in1=ST4[:, s, 0:4], op=ALU.mult),
             reads=[b], writes=[bSQ])
        P.op(dve, I("tensor_tensor", out=ST4[:, s, 4:8], in0=ST4[:, s, 4:8], in1=SQ_f[:, 0:4], op=ALU.subtract),
             reads=[b, bSQ], writes=[b])
        chk("vs_d")
        P.op(pool, I("tensor_tensor", out=ST4[:, s, 4:8], in0=ST4[:, s, 4:8], in1=NHALF[:, 0:4], op=ALU.pow),
             reads=[b], writes=[b], extra=[tNH])
        chk("vs_e")
        return s, b, zs

    def rotary(bank, m, tslot):
        KR32, bKR32 = KR32s[tslot], bKR32s[tslot]
        pv = BANK[bank][:, :].rearrange("p (h r d) -> p h r d", h=4, r=2)
        x1, x2 = pv[:, :, 0, :], pv[:, :, 1, :]
        cb = COS[:, m, :].unsqueeze(1).broadcast_to([128, 4, 64])
        sbb = SIN[:, m, :].unsqueeze(1).broadcast_to([128, 4, 64])
        T = TT[tslot]
        bT = bTT[tslot]
        P.group(dve, [I("tensor_tensor", out=T[j], in0=a, in1=b_, op=ALU.mult)
                      for j, (a, b_) in enumerate(((x1, cb), (x2, sbb), (x1, sbb), (x2, cb)))],
                reads=[bBANK[bank]], writes=[bT], extra=[TABTOK])
        P.group(pool, [I("tensor_tensor", out=KR32[:, :, 0, :], in0=T[0], in1=T[1], op=ALU.subtract),
                       I("tensor_tensor", out=KR32[:, :, 1, :], in0=T[2], in1=T[3], op=ALU.add)],
                reads=[bT], writes=[bKR32])

    for m in range(NT):
        i = m % 4
        P.dma(sp, XIN[i], x[m * 128:(m + 1) * 128, :], owner=bXIN[i], writes=[bXIN[i]])
        rms_stats(XIN[i], bXIN[i], HB[i], bHB[i], m)
        P.op(dve, I("scalar_tensor_tensor", out=HB[i], in0=XIN[i], scalar=RSTD[:, m:m + 1], in1=GB1,
                                                             op0=ALU.mult, op1=ALU.mult),
             reads=[bXIN[i], bSTc[m]], writes=[bHB[i]], extra=[TABTOK])
        transposes_to_HT(HB[i], bHB[i], m)

    done = {"flag": stop_after == "norm1"}

    def mixers(grp):
        hs = [4 * grp + h for h in range(4)]
        slot = load_wblock(w_in_v[:, :, OFF_V + 512 * grp: OFF_V + 512 * grp + 512])
        for m in range(NT):
            bk = next_pj()
            proj_group(slot, m, bk)
            P.op(act, I("activation", out=V[:, m, :], in_=BANK[bk][:, :], func=AF.Copy),
                 reads=[bBANK[bk]], writes=[bV[m]])
        chk("v")
        slot = load_wblock(w_in_v[:, :, OFF_K + 512 * grp: OFF_K + 512 * grp + 512])
        P.op(pool, I("memset", RST_f, 0.0), writes=[bRST])

        def k_post(m):
            ks = m % 3
            P.pegroup([I("transpose", out=bank_bf(3)[:, h, :], in_=KRB[ks][:, h * 128:(h + 1) * 128],
                                                          identity=IDENT[:]) for h in range(4)],
                      reads=[bKRB[ks]], writes=[bBANK[3]], extra=[TABTOK])
            P.op(act, I("activation", out=KT[:, :, m * 128:(m + 1) * 128], in_=bank_bf(3)[:, 0:4, :],
                                                  func=AF.Copy), reads=[bBANK[3]], writes=[bKT[m]])
            state_update(m)

        def state_update(m):
            P.pegroup([I("matmul", bank4(5)[:, h, :], lhsT=KZ[:, m, h * 128:(h + 1) * 128],
                                                    rhs=V[:, m, h * 128:(h + 1) * 128], start=True, stop=True)
                       for h in range(4)], reads=[bKZ[m], bV[m]], writes=[bBANK[5]])
            P.group(dve, [I("scalar_tensor_tensor", out=RST[:, h, :], in0=RST[:, h, :], scalar=gamma_c[hs[h]],
                                                                 in1=bank4(5)[:, h, :], op0=ALU.mult, op1=ALU.add)
                          for h in range(4)], reads=[bBANK[5]], writes=[bRST])

        for m in range(NT):
            bk = next_pj()
            proj_group(slot, m, bk)
            rotary(bk, m, m % 2)
            ks = m % 3
            KR32_f, bKR32 = KR32_fs[m % 2], bKR32s[m % 2]
            P.op(act, I("activation", out=KRB[ks], in_=KR32_f, func=AF.Copy), reads=[bKR32], writes=[bKRB[ks]])
            P.op(pool, I("tensor_tensor", out=KZ[:, m, :].rearrange("p (h d) -> p h d", h=4),
                                                      in0=KR32_f.rearrange("p (h d) -> p h d", h=4),
                                                      in1=TABS[:, hs[0]:hs[0] + 4].unsqueeze(2).broadcast_to([128, 4, 128]),
                                                      op=ALU.mult), reads=[bKR32], writes=[bKZ[m]], extra=[TABTOK])
            if m >= 2:
                k_post(m - 2)
        k_post(NT - 2)
        k_post(NT - 1)
        chk("k")
        P.dma(sp, ib[grp].ap()[:, :], RST_f, owner=bIB[grp], reads=[bRST], writes=[bIB[grp]])
        ccsem = P.sem(f"cc{grp}")
        if not nocc:
            pool.wait(bIB[grp].wtoks() + bOB[grp].alltoks())
            pool.ops.append(lambda e, grp=grp, ccsem=ccsem: e.collective_compute(
                "AllGather", ALU.bypass, replica_groups=[list(range(NCORES))],
                ins=[ib[grp].ap().opt()], outs=[ob[grp].ap().opt()]).then_inc(ccsem))
            tcc = (ccsem, 1, f"cc{grp}")
            bIB[grp].add_read(tcc)
            bOB[grp].add_write(tcc)
        chk("ag")
        slot = load_wblock(w_in_v[:, :, OFF_VS + 512 * grp: OFF_VS + 512 * grp + 512])
        for k in range(4):
            P.dma(sp, LNT[:, k, :], lnt_d[k:k + 1, 512 * grp:512 * grp + 512].broadcast_to([128, 512]), owner=bLNT,
                  writes=[bLNT] if k == 0 else [])
        bLNT.add_write((bLNT.dsem, 16 * bLNT.dcnt, "d_" + bLNT.name))
        lng = LNT[:, 0, :].rearrange("p (h d) -> p h d", h=4)
        lnb = LNT[:, 1, :].rearrange("p (h d) -> p h d", h=4)
        gng = LNT[:, 2, :].rearrange("p (h d) -> p h d", h=4)
        gnb = LNT[:, 3, :].rearrange("p (h d) -> p h d", h=4)

        def normalize(bank, s, b, zs):
            Z, bZ = Zs[zs], bZs[zs]
            P.group(dve, [I("tensor_scalar", out=Z[:, h, :], in0=Z[:, h, :],
                                                          scalar1=ST4[:, s, h:h + 1], scalar2=ST4[:, s, 4 + h:5 + h],
                                                          op0=ALU.subtract, op1=ALU.mult) for h in range(4)],
                    reads=[b], writes=[bZ])

        for m in range(NT):
            bk = next_pj()
            proj_group(slot, m, bk)
            s, b, zs = group_stats(bk, EPSC[:])
            dve.wait([tEPS])
            normalize(bk, s, b, zs)
            chk("vs_f")
            Z, bZ = Zs[zs], bZs[zs]
            P.op(dve, I("tensor_tensor", out=Z, in0=Z, in1=lng, op=ALU.mult), reads=[bZ, bLNT], writes=[bZ])
            P.op(dve, I("tensor_tensor", out=VN[:, m, :].rearrange("p (h d) -> p h d", h=4), in0=Z, in1=lnb,
                                                     op=ALU.add), reads=[bZ, bLNT], writes=[bVN[m]])
        chk("vs")
        slot = load_wblock(w_in_v[:, :, OFF_U + 512 * grp: OFF_U + 512 * grp + 512])
        for m in range(NT):
            i = m % 2
            P.pegroup([I("matmul", bank4(6)[:, h, :], lhsT=WST[:, hs[h], :],
                                                    rhs=VN[:, m, h * 128:(h + 1) * 128], start=True, stop=True)
                       for h in range(4)], reads=[bWST, bVN[m]], writes=[bBANK[6]])
            P.group(act, [I("activation", out=MB[i][:, h, :], in_=bank4(6)[:, h, :], func=AF.Identity,
                                                            bias=TABS[:, 24 + hs[h]:25 + hs[h]], scale=1.0)
                          for h in range(4)], reads=[bBANK[6]], writes=[bMB[i]], extra=[TABTOK])
            bk = next_pj()
            proj_group(slot, m, bk)
            P.op(dve, I("tensor_tensor",
                out=MIXED[:, m, 512 * grp:512 * grp + 512], in0=BANK[bk][:, :],
                in1=MB[i].rearrange("p h d -> p (h d)"), op=ALU.mult), reads=[bBANK[bk], bMB[i]], writes=[bMIX[m]])
        chk("sgu")
        slot = load_wblock(w_in_v[:, :, OFF_G + 512 * grp: OFF_G + 512 * grp + 512])
        for m in range(NT):
            bk = next_pj()
            proj_group(slot, m, bk)
            P.op(act, I("activation", out=G[:, m, :], in_=BANK[bk][:, :], func=AF.Silu),
                 reads=[bBANK[bk]], writes=[bG[m]])
        chk("g")
        slot = load_wblock(w_in_v[:, :, OFF_Q + 512 * grp: OFF_Q + 512 * grp + 512])

        def q_post(m):
            ks = m % 3
            P.pegroup([I("transpose", out=bank_bf(4)[:, h, :], in_=KRB[ks][:, h * 128:(h + 1) * 128],
                                                          identity=IDENT[:]) for h in range(4)],
                      reads=[bKRB[ks]], writes=[bBANK[4]], extra=[TABTOK])
            P.op(act, I("activation", out=QT[:, :, m * 128:(m + 1) * 128], in_=bank_bf(4)[:, 0:4, :],
                                                  func=AF.Copy), reads=[bBANK[4]], writes=[bQT[m]])

        for m in range(NT):
            bk = next_pj()
            proj_group(slot, m, bk)
            rotary(bk, m, m % 2)
            ks = m % 3
            KR32_f, bKR32 = KR32_fs[m % 2], bKR32s[m % 2]
            P.op(act, I("activation", out=KRB[ks], in_=KR32_f, func=AF.Copy), reads=[bKR32], writes=[bKRB[ks]])
            if m >= 2:
                q_post(m - 2)
        q_post(NT - 2)
        q_post(NT - 1)
        chk("q")
        for piece in range(4):
            half = piece % 2
            if nocc:
                P.dma(sp, AGL[half][:, 0].rearrange("p h d -> p (h d)"), ib[grp].ap()[:, :],
                      owner=bAGL[half], reads=[bIB[grp]], writes=[bAGL[half]])
                P.dma(sp, AGL[half][:, 1].rearrange("p h d -> p (h d)"), ib[grp].ap()[:, :],
                      owner=bAGL[half], reads=[bIB[grp]], writes=[bAGL[half]])
            else:
                P.dma(sp, AGL[half].rearrange("p r h d -> p r (h d)"),
                      ob[grp].ap()[piece * 256:(piece + 1) * 256, :].rearrange("(r p) f -> p r f", p=128),
                      owner=bAGL[half], reads=[bOB[grp]], writes=[bAGL[half]])
            for r in range(2):
                cp = piece * 2 + r
                if cp == 0:
                    fns = [I("tensor_scalar",
                        out=RST[:, h, :], in0=AGL[half][:, r, h, :], scalar1=COEF[:, hs[h]:hs[h] + 1], scalar2=None,
                        op0=ALU.mult) for h in range(4)]
                else:
                    fns = [I("scalar_tensor_tensor",
                        out=RST[:, h, :], in0=AGL[half][:, r, h, :], scalar=COEF[:, cp * 8 + hs[h]:cp * 8 + hs[h] + 1],
                        in1=RST[:, h, :], op0=ALU.mult, op1=ALU.add) for h in range(4)]
                P.group(dve, fns, reads=[bAGL[half]], writes=[bRST], extra=[TABTOK])
        chk("comb")
        P.op(act, I("activation", out=RN[0].rearrange("p h d -> p (h d)"), in_=RST_f, func=AF.Copy),
             reads=[bRST], writes=[bRN[0]])
        for n in range(NT):
            i = n % 2
            P.pegroup([I("matmul", bank4(6)[:, h, :], lhsT=KT[:, h, n * 128:(n + 1) * 128],
                                                    rhs=QT[:, h, n * 128:(n + 1) * 128], start=True, stop=True)
                       for h in range(4)], reads=[bKT[n], bQT[n]], writes=[bBANK[6]])
            if n < NT - 1:
                state_update(n)
                P.op(act, I("activation", out=RN[1 - i].rearrange("p h d -> p (h d)"), in_=RST_f, func=AF.Copy),
                     reads=[bRST], writes=[bRN[1 - i]])
            P.group(dve, [I("scalar_tensor_tensor",
                out=STB[i][:, h, :], in0=bank4(6)[:, h, :], scalar=TABS[:, 8 + hs[h]:9 + hs[h]], in1=MASK[:],
                op0=ALU.mult, op1=ALU.mult) for h in range(4)], reads=[bBANK[6]], writes=[bSTB[i]], extra=[TABTOK])
            fns = []
            for h in range(4):
                fns.append(I("matmul", bank4(7)[:, h, :], lhsT=STB[i][:, h, :],
                                                             rhs=V[:, n, h * 128:(h + 1) * 128], start=True, stop=False))
                fns.append(I("matmul", bank4(7)[:, h, :], lhsT=QT[:, h, n * 128:(n + 1) * 128],
                                                             rhs=RN[i][:, h, :], start=False, stop=True))
            P.pegroup(fns, reads=[bSTB[i], bV[n], bQT[n], bRN[i]], writes=[bBANK[7]])
            s, b, zs = group_stats(7, TABS[:, 16 + hs[0]:16 + hs[0] + 4])
            normalize(7, s, b, zs)
            Z, Z_f, bZ = Zs[zs], Z_fs[zs], bZs[zs]
            P.op(dve, I("tensor_tensor", out=Z, in0=Z, in1=gng, op=ALU.mult), reads=[bZ, bLNT], writes=[bZ])
            P.op(dve, I("tensor_tensor", out=Z, in0=Z, in1=gnb, op=ALU.add), reads=[bZ, bLNT], writes=[bZ])
            P.op(dve, I("tensor_tensor",
                out=MIXED[:, n, 1024 + 512 * grp:1024 + 512 * grp + 512], in0=Z_f, in1=G[:, n, :], op=ALU.mult),
                reads=[bZ, bG[n]], writes=[bMIX[n]])

    try:
        for grp in range((1 if stop_after == "mix0" else 2) if not done["flag"] else 0):
            mixers(grp)
    except _Stop:
        done["flag"] = True

    if stop_after in ("mix", "mix0"):
        done["flag"] = True

    if not done["flag"]:
        for m in range(NT):
            transposes_to_HT(MIXED[:, m, :], bMIX[m], m)
        phaseA_toks = []
        for b in phaseA_bufs:
            phaseA_toks += b.alltoks()
        for m in range(NT):
            P.dma(sp, X1[:, m, :], x[m * 128:(m + 1) * 128, :], owner=bX1[m], writes=[bX1[m]], extra=phaseA_toks)
        P.dma(sp, GB2, n2g[0:1, :].broadcast_to([128, D]), owner=bGB2, writes=[bGB2], extra=phaseA_toks)
        for cb in range(4):
            slot = load_wblock(w_out_v[:, :, cb * 512:(cb + 1) * 512])
            for m in range(NT):
                bk = next_pj()
                proj_group(slot, m, bk)
                P.op(dve, I("tensor_tensor",
                    out=X1[:, m, cb * 512:(cb + 1) * 512], in0=BANK[bk][:, :], in1=X1[:, m, cb * 512:(cb + 1) * 512],
                    op=ALU.add), reads=[bBANK[bk], bX1[m]], writes=[bX1[m]])
    if stop_after == "oproj":
        done["flag"] = True

    if not done["flag"]:
        for eng_ in (act, dve, pool):
            eng_.wait(phaseA_toks)
        for m in range(NT):
            i = m % 4
            rms_stats(X1[:, m, :], bX1[m], HB2[i], bHB2[i], 8 + m)
            P.op(dve, I("scalar_tensor_tensor", out=HB2[i], in0=X1[:, m, :], scalar=RSTD[:, 8 + m:9 + m],
                                                                 in1=GB2, op0=ALU.mult, op1=ALU.mult),
                 reads=[bX1[m], bSTc[8 + m], bGB2], writes=[bHB2[i]])
            transposes_to_HT(HB2[i], bHB2[i], m)
        ffn_first = list(phaseA_toks)
        for b in bWB:
            ffn_first += b.alltoks()
        nfb = NFB if stop_after != "ffn1" else 1

        def load_gu(q):
            fb, sbk = divmod(q, 2)
            if fb >= nfb:
                return
            slot = q % 2
            f0 = fb * 512 + sbk * 256
            P.dma(pool, WG[slot], w_gate_v[:, :, f0:f0 + 256], owner=bWGU[slot], writes=[bWGU[slot]], extra=ffn_first)
            P.dma(pool, WU[slot], w_up_v[:, :, f0:f0 + 256], owner=bWGU[slot], writes=[], extra=ffn_first)
            t = (bWGU[slot].dsem, 16 * bWGU[slot].dcnt, "d_" + bWGU[slot].name)
            bWGU[slot].add_write(t)
            return slot

        def load_wd(fb):
            if fb >= nfb:
                return
            slot = fb % 2
            P.dma(pool, WD[slot], w_down_v[:, 4 * fb:4 * fb + 4, :], owner=bWD[slot], writes=[bWD[slot]], extra=ffn_first)

        gu_pair = {"n": 0}

        def gu_units(fb):
            for sbk in range(2):
                wslot = (2 * fb + sbk) % 2
                for j in range(2):
                    fc = sbk * 2 + j
                    for hh in range(2):
                        pr = gu_pair["n"] % 2
                        gu_pair["n"] += 1
                        bg, bu = 2 * pr, 2 * pr + 1
                        fns = [I("matmul",
                            BANK[bg][:, :], lhsT=WG[wslot][:, c, j * 128:(j + 1) * 128],
                            rhs=HT[:, c, hh * 512:(hh + 1) * 512], start=(c == 0), stop=(c == DC - 1)) for c in range(DC)]
                        fns += [I("matmul",
                            BANK[bu][:, :], lhsT=WU[wslot][:, c, j * 128:(j + 1) * 128],
                            rhs=HT[:, c, hh * 512:(hh + 1) * 512], start=(c == 0), stop=(c == DC - 1)) for c in range(DC)]
                        P.pegroup(fns, reads=[bWGU[wslot]] + [bHT[4 * hh + k] for k in range(4)],
                                  writes=[bBANK[bg], bBANK[bu]])
                        si = gu_pair["n"] % 2
                        P.op(act, I("activation", out=SG[si], in_=BANK[bg][:, :], func=AF.Silu),
                             reads=[bBANK[bg]], writes=[bSG[si]], extra=phaseA_toks)
                        first = (fc == 0 and hh == 0)
                        t = P.op(dve, I("tensor_tensor",
                            out=AT[fb % 2][:, fc, hh * 512:(hh + 1) * 512], in0=BANK[bu][:, :], in1=SG[si], op=ALU.mult),
                            reads=[bBANK[bu], bSG[si]], writes=[bAT[fb % 2]] if first else [])
                        if not first:
                            bAT[fb % 2].add_write(t)
                load_gu(2 * fb + sbk + 2)

        dn_bank = {"n": 0}

        def dn_units(fb):
            a = fb % 2
            for m in range(NT):
                for cb in range(4):
                    bk = 4 + dn_bank["n"] % 4
                    dn_bank["n"] += 1
                    fns = [I("matmul",
                        BANK[bk][:, :], lhsT=AT[a][:, fc, m * 128:(m + 1) * 128], rhs=WD[a][:, fc, cb * 512:(cb + 1) * 512],
                        start=(fc == 0), stop=(fc == 3)) for fc in range(4)]
                    P.pegroup(fns, reads=[bAT[a], bWD[a]], writes=[bBANK[bk]])
                    P.op(dve, I("tensor_tensor",
                        out=X1[:, m, cb * 512:(cb + 1) * 512], in0=BANK[bk][:, :], in1=X1[:, m, cb * 512:(cb + 1) * 512],
                        op=ALU.add), reads=[bBANK[bk], bX1[m]], writes=[bX1[m]])

        load_gu(0)
        load_gu(1)
        load_wd(0)
        load_wd(1)
        gu_units(0)
        for fb in range(1, nfb):
            gu_units(fb)
            dn_units(fb - 1)
            load_wd(fb + 1)
        dn_units(nfb - 1)
        P.dma(sp, GB2, nfg[0:1, :].broadcast_to([128, D]), owner=bGB2, writes=[bGB2])
        for m in range(NT):
            i = m % 2
            rms_stats(X1[:, m, :], bX1[m], HB2[i], bHB2[i], 16 + m)
            P.op(dve, I("scalar_tensor_tensor", out=X1[:, m, :], in0=X1[:, m, :], scalar=RSTD[:, 16 + m:17 + m],
                                                            in1=GB2, op0=ALU.mult, op1=ALU.mult),
                 reads=[bX1[m], bSTc[16 + m], bGB2], writes=[bX1[m]])
            P.dma(sp, out[m * 128:(m + 1) * 128, :], X1[:, m, :], owner=bX1[m], reads=[bX1[m]])

    for name, getter, shape, dt in dumps:
        dten = nc.dram_tensor("dump_" + name, shape, dt, kind="ExternalOutput").ap()
        ap = getter(locals())
        bd = P.buf("dump_" + name)
        alltoks = []
        for b in P.bufs:
            alltoks += b.alltoks()
        P.dma(sp, dten, ap, owner=bd, extra=alltoks, reads=[bd])

    final = []
    for b in P.bufs:
        final += b.alltoks()
    sp.wait(final)

    with nc.Block() as block:
        for ename, q in (("sync", sp), ("scalar", act), ("vector", dve), ("gpsimd", pool), ("tensor", pe)):
            def body(e, q=q):
                for fn in q.ops:
                    fn(e)
            getattr(block, ename)(body)
    es.close()
    return nc


def make_in_maps(x, norm1_g, w_in, sgu_ln_g, sgu_ln_b, w_spatial, b_spatial, ret_gn_g, ret_gn_b,
                 w_out, norm2_g, w_gate, w_up, w_down, final_norm_g):
    f32 = lambda a: np.ascontiguousarray(np.asarray(a), dtype=np.float32)
    zeta, gneg, eps2, _, mask, ident = _const_tables()
    tabs = np.concatenate([zeta, gneg, eps2, f32(b_spatial)[0].T.astype(np.float64)], axis=1).astype(np.float32)
    shared = {
        "w_in": f32(w_in)[0], "w_out": f32(w_out)[0], "w_gate": f32(w_gate)[0], "w_up": f32(w_up)[0],
        "w_down": f32(w_down)[0],
        "n1g": f32(norm1_g).reshape(1, D), "n2g": f32(norm2_g).reshape(1, D), "nfg": f32(final_norm_g).reshape(1, D),
        "lnt": np.stack([f32(sgu_ln_g).reshape(-1), f32(sgu_ln_b).reshape(-1), f32(ret_gn_g).reshape(-1),
                         f32(ret_gn_b).reshape(-1)], axis=0),
        "wsp": f32(w_spatial)[0],
        "tabs": np.ascontiguousarray(tabs),
        "ident": ident.astype(ml_dtypes.bfloat16), "mask": mask.astype(ml_dtypes.bfloat16),
    }
    xf = f32(x)[0]
    in_maps = []
    for c in range(NCORES):
        cos, sin, coef = _core_tables(c)
        m = dict(shared)
        m["x"] = np.ascontiguousarray(xf[c * TPC:(c + 1) * TPC])
        m["cos"], m["sin"], m["coef"] = cos, sin, coef
        in_maps.append(m)
    return in_maps


_NC_CACHE = {}


def kernel(**inputs):
    in_maps = make_in_maps(**inputs)
    if "nc" not in _NC_CACHE:
        _NC_CACHE["nc"] = build_program()
    res = run_bass_kernel_spmd(_NC_CACHE["nc"], in_maps, core_ids=list(range(NCORES)))
    return np.concatenate([r["out"] for r in res.results], axis=0).reshape(1, SEQ, D).astype(np.float32)
```

```python
import contextlib
import numpy as np
import ml_dtypes
import concourse.bass as bass
import concourse.mybir as mybir
from concourse.bass_utils import run_bass_kernel_spmd

F32 = mybir.dt.float32
BF16 = mybir.dt.bfloat16
AF = mybir.ActivationFunctionType
ALU = mybir.AluOpType
AX = mybir.AxisListType

NCORES = 8
SEQ = 8192
D = 2048
TPC = SEQ // NCORES
NT = TPC // 128
DC = D // 128
INW = 6144
DFF = 5632
NFB = DFF // 512
EPS = 1e-6
OFF_U, OFF_VS, OFF_Q, OFF_K, OFF_V, OFF_G = 0, 1024, 2048, 3072, 4096, 5120


def I(name, *args, **kw):
    return lambda e: getattr(e, name)(*args, **kw)


class Eng:
    def __init__(self, name, sem):
        self.name = name
        self.sem = sem
        self.ops = []
        self.cnt = 0
        self.seen = {}

    def wait(self, toks):
        best = {}
        for t in toks:
            if t is None:
                continue
            if t[2] not in best or best[t[2]][1] < t[1]:
                best[t[2]] = t
        for sem, val, key in best.values():
            if self.seen.get(key, 0) >= val:
                continue
            self.seen[key] = val
            self.ops.append(I("wait_ge", sem, val))

    def emit(self, fn, sig=True):
        if sig:
            self.cnt += 1
            c = self.cnt
            sem = self.sem
            self.ops.append(lambda e, fn=fn, sem=sem: fn(e).then_inc(sem, 1))
            return (self.sem, c, self.name)
        self.ops.append(lambda e, fn=fn: fn(e))
        return None


class Buf:
    def __init__(self, name):
        self.name = name
        self.w = {}
        self.r = {}
        self.dsem = None
        self.dcnt = 0

    def wtoks(self):
        return list(self.w.values())

    def rtoks(self):
        return list(self.r.values())

    def alltoks(self):
        return self.wtoks() + self.rtoks()

    def add_read(self, t):
        k = t[2]
        if k not in self.r or self.r[k][1] < t[1]:
            self.r[k] = t

    def add_write(self, t):
        self.w = {t[2]: t}
        self.r = {}


class Prog:
    def __init__(self, nc, es):
        self.nc = nc
        self.es = es
        self.nsem = 0
        self.act = Eng("act", self.sem("s_act"))
        self.dve = Eng("dve", self.sem("s_dve"))
        self.pool = Eng("pool", self.sem("s_pool"))
        self.pe = Eng("pe", self.sem("s_pe"))
        self.sp = Eng("sp", self.sem("s_sp"))
        self.bufs = []

    def sem(self, name):
        self.nsem += 1
        return self.es.enter_context(self.nc.semaphore(name))

    def buf(self, name):
        b = Buf(name)
        self.bufs.append(b)
        return b

    def _deps(self, reads, writes, extra):
        toks = list(extra)
        for b in reads:
            toks += b.wtoks()
        for b in writes:
            toks += b.wtoks() + b.rtoks()
        return toks

    def op(self, eng, fn, reads=(), writes=(), extra=()):
        eng.wait(self._deps(reads, writes, extra))
        t = eng.emit(fn, sig=True)
        for b in reads:
            b.add_read(t)
        for b in writes:
            b.add_write(t)
        return t

    def pegroup(self, fns, reads=(), writes=(), extra=()):
        return self.group(self.pe, fns, reads, writes, extra)

    def group(self, eng, fns, reads=(), writes=(), extra=()):
        eng.wait(self._deps(reads, writes, extra))
        for fn in fns[:-1]:
            eng.emit(fn, sig=False)
        t = eng.emit(fns[-1], sig=True)
        for b in reads:
            b.add_read(t)
        for b in writes:
            b.add_write(t)
        return t

    def dma(self, q, out, in_, owner, reads=(), writes=(), extra=()):
        if owner.dsem is None:
            owner.dsem = self.sem("d_" + owner.name)
        q.wait(self._deps(reads, writes, extra))
        owner.dcnt += 1
        sem, val = owner.dsem, 16 * owner.dcnt
        q.ops.append(lambda e, out=out, in_=in_, sem=sem: e.dma_start(out=out, in_=in_).then_inc(sem, 16))
        t = (sem, val, "d_" + owner.name)
        for b in reads:
            b.add_read(t)
        for b in writes:
            b.add_write(t)
        return t


def _log_gamma():
    h = np.arange(8, dtype=np.float64)
    return np.log(1.0 - np.exp2(-5.0 - h))


def _const_tables():
    lg = _log_gamma()
    t = np.arange(128, dtype=np.float64)
    s = 128.0 ** -0.5
    zeta = s * np.exp((128.0 - t)[:, None] * lg[None, :])
    gneg = s * np.exp(-t[:, None] * lg[None, :])
    eps2 = EPS * np.exp(-2.0 * t[:, None] * lg[None, :])
    gamma_c = np.exp(128.0 * lg)
    mask = (t[:, None] <= t[None, :]).astype(np.float32)
    ident = np.eye(128, dtype=np.float32)
    return zeta, gneg, eps2, gamma_c, mask, ident


def _core_tables(core):
    lg = _log_gamma()
    half = 64
    inv = 1.0 / (10000.0 ** (np.arange(half, dtype=np.float64) / half))
    pos = core * TPC + np.arange(TPC, dtype=np.float64)
    ang = pos[:, None] * inv[None, :]
    cos = np.cos(ang).reshape(NT, 128, half).transpose(1, 0, 2)
    sin = np.sin(ang).reshape(NT, 128, half).transpose(1, 0, 2)
    coef = np.zeros((8, 8), dtype=np.float64)
    for cp in range(core):
        coef[cp, :] = np.exp(1024.0 * (core - 1 - cp) * lg)
    return (np.ascontiguousarray(cos, dtype=np.float32), np.ascontiguousarray(sin, dtype=np.float32),
            coef.reshape(1, 64).astype(np.float32))


def build_program(stop_after=None, dumps=(), nocc=False):
    nc = bass.Bass("TRN2", target_bir_lowering=False)
    es = contextlib.ExitStack()
    _, _, _, gamma_c, _, _ = _const_tables()
    gamma_c = [float(np.float32(v)) for v in gamma_c]

    def din(name, shape, dt=F32):
        return nc.dram_tensor(name, shape, dt, kind="ExternalInput").ap()

    x = din("x", [TPC, D])
    w_in = din("w_in", [D, INW])
    w_out = din("w_out", [D, D])
    lite = stop_after is not None and stop_after not in ("ffn1",)
    w_gate = din("w_gate", [D, DFF] if not lite else [128, 128])
    w_up = din("w_up", [D, DFF] if not lite else [128, 128])
    w_down = din("w_down", [DFF, D] if not lite else [128, 128])
    n1g = din("n1g", [1, D])
    n2g = din("n2g", [1, D])
    nfg = din("nfg", [1, D])
    lnt_d = din("lnt", [4, 1024])
    wsp_d = din("wsp", [8, 128, 128])
    tabs_d = din("tabs", [128, 32])
    cos_d = din("cos", [128, NT, 64])
    sin_d = din("sin", [128, NT, 64])
    coef_d = din("coef", [1, 64])
    gm_d = din("gm", [128, 8, 128])
    ident_d = din("ident", [128, 128], BF16)
    mask_d = din("mask", [128, 128], BF16)
    out = nc.dram_tensor("out", [TPC, D], F32, kind="ExternalOutput").ap()
    ib = [nc.dram_tensor(f"ib{g}", [128, 512], F32) for g in range(2)]
    ob = [nc.dram_tensor(f"ob{g}", [NCORES * 128, 512], F32) for g in range(2)]
    dump_out = {}

    w_in_v = w_in.rearrange("(c p) e -> p c e", p=128)
    w_out_v = w_out.rearrange("(c p) e -> p c e", p=128)
    w_gate_v = w_gate.rearrange("(c p) f -> p c f", p=128)
    w_up_v = w_up.rearrange("(c p) f -> p c f", p=128)
    w_down_v = w_down.rearrange("(c p) e -> p c e", p=128)

    def sb(name, shape, dt):
        return es.enter_context(nc.sbuf_tensor("s_" + name, shape, dt))

    P = Prog(nc, es)
    act, dve, pool, pe, sp = P.act, P.dve, P.pool, P.pe, P.sp

    ARA = sb("arena_a", [128, 32768], BF16)
    HT = sb("ht", [128, DC, TPC], BF16)
    WBT = [sb(f"wb{i}", [128, 8192], BF16) for i in range(2)]
    ARB = sb("arena_b", [128, 39680], BF16)
    IDENT = sb("ident", [128, 128], BF16)
    MASK = sb("maskt", [128, 128], BF16)
    STAT = sb("stat", [128, 24], F32)
    RSTD = sb("rstd", [128, 24], F32)
    NHALF = sb("nhalf", [128, 8], F32)
    ST4 = sb("st4", [128, 8, 8], F32)

    def carve(arena, off, n, dt=BF16):
        if dt == BF16:
            return arena[:, off:off + n], off + n
        assert dt == F32
        return arena[:, off:off + 2 * n].bitcast(F32), off + 2 * n

    o = 0
    MIXED_f, o = carve(ARA, o, NT * 2048)
    MIXED = MIXED_f.rearrange("p (m d) -> p m d", m=NT)
    KZ_f, o = carve(ARA, o, NT * 512)
    KZ = KZ_f.rearrange("p (m d) -> p m d", m=NT)
    QT_f, o = carve(ARA, o, 4 * TPC)
    QT = QT_f.rearrange("p (h t) -> p h t", h=4)
    G_f, o = carve(ARA, o, NT * 512)
    G = G_f.rearrange("p (m d) -> p m d", m=NT)
    V_f, o = carve(ARA, o, NT * 512)
    V = V_f.rearrange("p (m d) -> p m d", m=NT)
    assert o == 32768
    o = 0
    XIN = []
    for i in range(2):
        a, o = carve(ARA, o, 2048, F32)
        XIN.append(a)
    GB1, o = carve(ARA, o, 2048, F32)
    HB = []
    for i in range(2):
        a, o = carve(ARA, o, 2048)
        HB.append(a)
    o = 16384
    for i in range(2):
        a, o = carve(ARA, o, 2048, F32)
        XIN.append(a)
    for i in range(2):
        a, o = carve(ARA, o, 2048)
        HB.append(a)
    X1 = ARA[:, :].bitcast(F32).rearrange("p (m d) -> p m d", m=NT)

    o = 0
    KT_f, o = carve(ARB, o, 4 * TPC)
    KT = KT_f.rearrange("p (h t) -> p h t", h=4)
    VN_f, o = carve(ARB, o, NT * 512)
    VN = VN_f.rearrange("p (m d) -> p m d", m=NT)
    MB = []
    for i in range(2):
        a, o = carve(ARB, o, 512, F32)
        MB.append(a.rearrange("p (h d) -> p h d", h=4))
    COS_f, o = carve(ARB, o, NT * 64, F32)
    COS = COS_f.rearrange("p (m d) -> p m d", m=NT)
    SIN_f, o = carve(ARB, o, NT * 64, F32)
    SIN = SIN_f.rearrange("p (m d) -> p m d", m=NT)
    LNT_f, o = carve(ARB, o, 2048, F32)
    LNT = LNT_f.rearrange("p (k d) -> p k d", k=4)
    WST_f, o = carve(ARB, o, 1024)
    WST = WST_f.rearrange("p (h t) -> p h t", h=8)
    WSP_f, o = carve(ARB, o, 1024)
    WSP = WSP_f.rearrange("p (h s) -> p h s", h=8)
    TABS, o = carve(ARB, o, 32, F32)
    COEF, o = carve(ARB, o, 64, F32)
    RST_f, o = carve(ARB, o, 512, F32)
    RST = RST_f.rearrange("p (h d) -> p h d", h=4)
    RN = []
    for i in range(2):
        a, o = carve(ARB, o, 512)
        RN.append(a.rearrange("p (h d) -> p h d", h=4))
    AGL = []
    for i in range(2):
        a, o = carve(ARB, o, 2 * 512, F32)
        AGL.append(a.rearrange("p (r h d) -> p r h d", r=2, h=4))
    TT = []
    for i in range(2):
        tl = []
        for j in range(4):
            a, o = carve(ARB, o, 256, F32)
            tl.append(a.rearrange("p (h d) -> p h d", h=4))
        TT.append(tl)
    KR32_fs, KR32s = [], []
    for i in range(2):
        a, o = carve(ARB, o, 512, F32)
        KR32_fs.append(a)
        KR32s.append(a.rearrange("p (h r d) -> p h r d", h=4, r=2))
    KRB = []
    for i in range(3):
        a, o = carve(ARB, o, 512)
        KRB.append(a)
    SQ_fs, SQs, Z_fs, Zs = [], [], [], []
    for i in range(2):
        a, o = carve(ARB, o, 512, F32)
        SQ_fs.append(a)
        SQs.append(a.rearrange("p (h d) -> p h d", h=4))
        a, o = carve(ARB, o, 512, F32)
        Z_fs.append(a)
        Zs.append(a.rearrange("p (h d) -> p h d", h=4))
    STB = []
    for i in range(2):
        a, o = carve(ARB, o, 512)
        STB.append(a.rearrange("p (h d) -> p h d", h=4))
    GMT_f, o = carve(ARB, o, 512, F32)
    GMT = GMT_f.rearrange("p (h d) -> p h d", h=4)
    assert o <= 39680, o
    o = 0
    WD = []
    for i in range(2):
        a, o = carve(ARB, o, 8192)
        WD.append(a.rearrange("p (c e) -> p c e", c=4))
    AT = []
    for i in range(2):
        a, o = carve(ARB, o, 4096)
        AT.append(a.rearrange("p (c t) -> p c t", c=4))
    SG = []
    for i in range(2):
        a, o = carve(ARB, o, 512)
        SG.append(a)
    HB2 = []
    for i in range(4):
        a, o = carve(ARB, o, 2048)
        HB2.append(a)
    GB2, o = carve(ARB, o, 2048, F32)
    assert o <= 39680, o

    WB = [w[:, :].rearrange("p (c e) -> p c e", c=DC) for w in WBT]
    WG = [w[:, 0:4096].rearrange("p (c e) -> p c e", c=DC) for w in WBT]
    WU = [w[:, 4096:8192].rearrange("p (c e) -> p c e", c=DC) for w in WBT]

    BANK = [es.enter_context(nc.psum_tensor(f"bank{i}", [128, 512], F32)) for i in range(8)]
    bBANK = [P.buf(f"bank{i}") for i in range(8)]

    def bank4(i):
        return BANK[i][:, :].rearrange("p (h d) -> p h d", h=4)

    def bank_bf(i):
        return BANK[i][:, :].bitcast(BF16).rearrange("p (g d) -> p g d", g=8)

    bTAB = P.buf("tab")
    bXIN = [P.buf(f"xin{i}") for i in range(4)]
    bHB = [P.buf(f"hb{i}") for i in range(4)]
    bGB = P.buf("gb")
    bHT = [P.buf(f"ht{m}") for m in range(NT)]
    bWB = [P.buf(f"wb{i}") for i in range(2)]
    bSTAT = P.buf("stat")
    bSTc = [P.buf(f"statc{i}") for i in range(24)]
    bV = [P.buf(f"v{m}") for m in range(NT)]
    bKZ = [P.buf(f"kz{m}") for m in range(NT)]
    bKT = [P.buf(f"kt{m}") for m in range(NT)]
    bQT = [P.buf(f"qt{m}") for m in range(NT)]
    bG = [P.buf(f"g{m}") for m in range(NT)]
    bVN = [P.buf(f"vn{m}") for m in range(NT)]
    bMB = [P.buf(f"mb{i}") for i in range(2)]
    bMIX = [P.buf(f"mix{m}") for m in range(NT)]
    bRST = P.buf("rst")
    bRN = [P.buf(f"rn{i}") for i in range(2)]
    bAGL = [P.buf(f"agl{i}") for i in range(2)]
    bTT = [P.buf(f"tt{i}") for i in range(2)]
    bKR32s = [P.buf(f"kr32_{i}") for i in range(2)]
    bLNT = P.buf("lnt")
    bGM = P.buf("gm")
    bKRB = [P.buf(f"krb{i}") for i in range(3)]
    bSQs = [P.buf(f"sq{i}") for i in range(2)]
    bZs = [P.buf(f"z{i}") for i in range(2)]
    bSTB = [P.buf(f"stb{i}") for i in range(2)]
    bST4 = [P.buf(f"st4_{i}") for i in range(8)]
    bWST = P.buf("wst")
    bWSP = P.buf("wsp")
    bIB = [P.buf(f"ib{g}") for g in range(2)]
    bOB = [P.buf(f"ob{g}") for g in range(2)]
    bX1 = [P.buf(f"x1_{m}") for m in range(NT)]
    bWD = [P.buf(f"wd{i}") for i in range(2)]
    bAT = [P.buf(f"at{i}") for i in range(2)]
    bSG = [P.buf(f"sg{i}") for i in range(2)]
    bWGU = [P.buf(f"wgu{i}") for i in range(2)]
    bHB2 = [P.buf(f"hb2_{i}") for i in range(4)]
    bGB2 = P.buf("gb2")
    phaseA_bufs = list(P.bufs)

    def tab_load(dst, src, q=None):
        P.dma(q or sp, dst, src, owner=bTAB, writes=[])
    n_tab = 0
    for dst, src in ((IDENT[:], ident_d[:, :]), (MASK[:], mask_d[:, :]), (COS_f, cos_d.rearrange("p m d -> p (m d)")),
                     (SIN_f, sin_d.rearrange("p m d -> p (m d)")), (TABS, tabs_d[:, :]),
                     (COEF, coef_d[0:1, :].broadcast_to([128, 64]))):
        tab_load(dst, src)
        n_tab += 1
    tab_load(GB1, n1g[0:1, :].broadcast_to([128, D]))
    n_tab += 1
    TABTOK = (bTAB.dsem, 16 * n_tab, "d_tab")
    P.dma(pool, WSP, wsp_d.rearrange("h t s -> t h s"), owner=bWSP, writes=[bWSP])
    P.op(pool, I("memset", STAT[:], 0.0), writes=[bSTAT])
    tNH = P.op(pool, I("memset", NHALF[:], -0.5))
    EPSC = sb("epsc", [128, 4], F32)
    tEPS = P.op(pool, I("memset", EPSC[:], EPS))
    P.pegroup([I("transpose", out=bank_bf(3)[:, h, :], in_=WSP[:, h, :], identity=IDENT[:]) for h in range(8)],
              reads=[bWSP], writes=[bBANK[3]], extra=[TABTOK])
    P.op(dve, I("tensor_tensor", out=WST, in0=bank_bf(3), in1=MASK[:].unsqueeze(1).broadcast_to([128, 8, 128]),
                                        op=ALU.mult), reads=[bBANK[3]], writes=[bWST], extra=[TABTOK])

    class _Stop(Exception):
        pass

    def chk(name):
        if stop_after == name:
            raise _Stop()

    wsrc = []
    for g_ in range(2):
        for off in (OFF_V, OFF_K, OFF_VS, OFF_U, OFF_G, OFF_Q):
            wsrc.append(w_in_v[:, :, off + 512 * g_: off + 512 * g_ + 512])
    for cb_ in range(4):
        wsrc.append(w_out_v[:, :, cb_ * 512:(cb_ + 1) * 512])
    wq = {"issued": 0, "cur": 0}

    def issue_next():
        i = wq["issued"]
        if i < len(wsrc):
            P.dma(pool, WB[i % 2], wsrc[i], owner=bWB[i % 2], writes=[bWB[i % 2]])
            wq["issued"] += 1

    def load_wblock(src_ap=None):
        i = wq["cur"]
        wq["cur"] += 1
        while wq["issued"] <= i + 1 and wq["issued"] < len(wsrc):
            issue_next()
        return i % 2

    pj = {"n": 0}

    def next_pj():
        b = pj["n"] % 3
        pj["n"] += 1
        return b

    def proj_group(slot, m, bank):
        fns = [I("matmul", BANK[bank][:, :], lhsT=HT[:, c, m * 128:(m + 1) * 128], rhs=WB[slot][:, c, :],
                                       start=(c == 0), stop=(c == DC - 1)) for c in range(DC)]
        return P.pegroup(fns, reads=[bHT[m], bWB[slot]], writes=[bBANK[bank]])

    def rms_stats(src_ap, src_buf, junk_ap, junk_buf, col):
        P.op(act, I("activation", out=junk_ap, in_=src_ap, func=AF.Square, accum_out=STAT[:, col:col + 1]),
             reads=[src_buf, bSTAT], writes=[junk_buf, bSTc[col]])
        P.op(dve, I("tensor_scalar", out=RSTD[:, col:col + 1], in0=STAT[:, col:col + 1], scalar1=1.0 / D,
                                            scalar2=EPS, op0=ALU.mult, op1=ALU.add), reads=[bSTc[col]], writes=[bSTc[col]])
        P.op(pool, I("tensor_tensor", out=RSTD[:, col:col + 1], in0=RSTD[:, col:col + 1], in1=NHALF[:, 0:1],
                                             op=ALU.pow), reads=[bSTc[col]], writes=[bSTc[col]], extra=[tNH])

    def transposes_to_HT(src_ap, src_buf, m):
        for half in range(2):
            bk = 3 + half
            P.pegroup([I("transpose", out=bank_bf(bk)[:, c % 8, :], in_=src_ap[:, c * 128:(c + 1) * 128],
                                                          identity=IDENT[:]) for c in range(half * 8, half * 8 + 8)],
                      reads=[src_buf], writes=[bBANK[bk]], extra=[TABTOK])
            P.op(act, I("activation", out=HT[:, half * 8:half * 8 + 8, m * 128:(m + 1) * 128],
                                                               in_=bank_bf(bk), func=AF.Copy),
                 reads=[bBANK[bk]], writes=[bHT[m]])

    st4 = {"n": 0}

    def group_stats(bank, eps_ap):
        s = st4["n"] % 8
        zs = st4["n"] % 2
        st4["n"] += 1
        b = bST4[s]
        Z, SQ, SQ_f, bZ, bSQ = Zs[zs], SQs[zs], SQ_fs[zs], bZs[zs], bSQs[zs]
        chk("vs_a")
        P.op(pool, I("memset", ST4[:, s, :], 0.0), writes=[b])
        fns = [I("activation", out=Z[:, h, :], in_=bank4(bank)[:, h, :], func=AF.Identity,
                 accum_out=ST4[:, s, h:h + 1]) for h in range(4)]
        fns += [I("activation", out=SQ[:, h, :], in_=bank4(bank)[:, h, :], func=AF.Square,
                  accum_out=ST4[:, s, 4 + h:5 + h]) for h in range(4)]
        P.group(act, fns, reads=[bBANK[bank]], writes=[bSQ, bZ, b])
        chk("vs_c")
        P.op(dve, I("tensor_scalar", out=ST4[:, s, 0:4], in0=ST4[:, s, 0:4], scalar1=1.0 / 128, scalar2=None,
                                            op0=ALU.mult), reads=[b], writes=[b])
        P.op(dve, I("scalar_tensor_tensor", out=ST4[:, s, 4:8], in0=ST4[:, s, 4:8], scalar=1.0 / 128,
                                                   in1=eps_ap, op0=ALU.mult, op1=ALU.add), reads=[b], writes=[b],
             extra=[TABTOK])
        P.op(dve, I("tensor_tensor", out=SQ_f[:, 0:4], in0=ST4[:, s, 0:4], in1=ST4[:, s, 0:4], op=ALU.mult),
             reads=[b], writes=[bSQ])
        P.op(dve, I("tensor_tensor", out=ST4[:, s, 4:8], in0=ST4[:, s, 4:8], in1=SQ_f[:, 0:4], op=ALU.subtract),
             reads=[b, bSQ], writes=[b])
        chk("vs_d")
        P.op(pool, I("tensor_tensor", out=ST4[:, s, 4:8], in0=ST4[:, s, 4:8], in1=NHALF[:, 0:4], op=ALU.pow),
             reads=[b], writes=[b], extra=[tNH])
        chk("vs_e")
        return s, b, zs

    def rotary(bank, m, tslot):
        KR32, bKR32 = KR32s[tslot], bKR32s[tslot]
        pv = BANK[bank][:, :].rearrange("p (h r d) -> p h r d", h=4, r=2)
        x1, x2 = pv[:, :, 0, :], pv[:, :, 1, :]
        cb = COS[:, m, :].unsqueeze(1).broadcast_to([128, 4, 64])
        sbb = SIN[:, m, :].unsqueeze(1).broadcast_to([128, 4, 64])
        T = TT[tslot]
        bT = bTT[tslot]
        P.group(dve, [I("tensor_tensor", out=T[j], in0=a, in1=b_, op=ALU.mult)
                      for j, (a, b_) in enumerate(((x1, cb), (x2, sbb), (x1, sbb), (x2, cb)))],
                reads=[bBANK[bank]], writes=[bT], extra=[TABTOK])
        P.group(pool, [I("tensor_tensor", out=KR32[:, :, 0, :], in0=T[0], in1=T[1], op=ALU.subtract),
                       I("tensor_tensor", out=KR32[:, :, 1, :], in0=T[2], in1=T[3], op=ALU.add)],
                reads=[bT], writes=[bKR32])

    for m in range(NT):
        i = m % 4
        P.dma(sp, XIN[i], x[m * 128:(m + 1) * 128, :], owner=bXIN[i], writes=[bXIN[i]])
        rms_stats(XIN[i], bXIN[i], HB[i], bHB[i], m)
        P.op(dve, I("scalar_tensor_tensor", out=HB[i], in0=XIN[i], scalar=RSTD[:, m:m + 1], in1=GB1,
                                                             op0=ALU.mult, op1=ALU.mult),
             reads=[bXIN[i], bSTc[m]], writes=[bHB[i]], extra=[TABTOK])
        transposes_to_HT(HB[i], bHB[i], m)

    done = {"flag": stop_after == "norm1"}

    def mixers(grp):
        hs = [4 * grp + h for h in range(4)]
        def v_proj_tile(slot_, m):
            bk = next_pj()
            proj_group(slot_, m, bk)
            P.op(act, I("activation", out=V[:, m, :], in_=BANK[bk][:, :], func=AF.Copy),
                 reads=[bBANK[bk]], writes=[bV[m]])

        if grp == 0:
            slot = load_wblock()
            for m in range(NT):
                v_proj_tile(slot, m)
        chk("v")
        slot = load_wblock(w_in_v[:, :, OFF_K + 512 * grp: OFF_K + 512 * grp + 512])
        P.op(pool, I("memset", RST_f, 0.0), writes=[bRST])

        def k_post(m):
            ks = m % 3
            P.pegroup([I("transpose", out=bank_bf(3)[:, h, :], in_=KRB[ks][:, h * 128:(h + 1) * 128],
                                                          identity=IDENT[:]) for h in range(4)],
                      reads=[bKRB[ks]], writes=[bBANK[3]], extra=[TABTOK])
            P.op(act, I("activation", out=KT[:, :, m * 128:(m + 1) * 128], in_=bank_bf(3)[:, 0:4, :],
                                                  func=AF.Copy), reads=[bBANK[3]], writes=[bKT[m]])
            state_update(m)

        def state_update(m):
            P.pegroup([I("matmul", bank4(5)[:, h, :], lhsT=KZ[:, m, h * 128:(h + 1) * 128],
                                                    rhs=V[:, m, h * 128:(h + 1) * 128], start=True, stop=True)
                       for h in range(4)], reads=[bKZ[m], bV[m]], writes=[bBANK[5]])
            P.group(dve, [I("scalar_tensor_tensor", out=RST[:, h, :], in0=RST[:, h, :], scalar=gamma_c[hs[h]],
                                                                 in1=bank4(5)[:, h, :], op0=ALU.mult, op1=ALU.add)
                          for h in range(4)], reads=[bBANK[5]], writes=[bRST])

        for m in range(NT):
            bk = next_pj()
            proj_group(slot, m, bk)
            rotary(bk, m, m % 2)
            ks = m % 3
            KR32_f, bKR32 = KR32_fs[m % 2], bKR32s[m % 2]
            P.op(act, I("activation", out=KRB[ks], in_=KR32_f, func=AF.Copy), reads=[bKR32], writes=[bKRB[ks]])
            P.op(pool, I("tensor_tensor", out=KZ[:, m, :].rearrange("p (h d) -> p h d", h=4),
                                                      in0=KR32_f.rearrange("p (h d) -> p h d", h=4),
                                                      in1=TABS[:, hs[0]:hs[0] + 4].unsqueeze(2).broadcast_to([128, 4, 128]),
                                                      op=ALU.mult), reads=[bKR32], writes=[bKZ[m]], extra=[TABTOK])
            if m >= 2:
                k_post(m - 2)
        k_post(NT - 2)
        k_post(NT - 1)
        chk("k")
        P.dma(sp, ib[grp].ap()[:, :], RST_f, owner=bIB[grp], reads=[bRST], writes=[bIB[grp]])
        ccsem = P.sem(f"cc{grp}")
        if not nocc:
            pool.wait(bIB[grp].wtoks() + bOB[grp].alltoks())
            pool.ops.append(lambda e, grp=grp, ccsem=ccsem: e.collective_compute(
                "AllGather", ALU.bypass, replica_groups=[list(range(NCORES))],
                ins=[ib[grp].ap().opt()], outs=[ob[grp].ap().opt()]).then_inc(ccsem))
            tcc = (ccsem, 1, f"cc{grp}")
            bIB[grp].add_read(tcc)
            bOB[grp].add_write(tcc)
        chk("ag")
        slot = load_wblock(w_in_v[:, :, OFF_VS + 512 * grp: OFF_VS + 512 * grp + 512])
        for k in range(4):
            P.dma(sp, LNT[:, k, :], lnt_d[k:k + 1, 512 * grp:512 * grp + 512].broadcast_to([128, 512]), owner=bLNT,
                  writes=[bLNT] if k == 0 else [])
        bLNT.add_write((bLNT.dsem, 16 * bLNT.dcnt, "d_" + bLNT.name))
        lng = LNT[:, 0, :].rearrange("p (h d) -> p h d", h=4)
        lnb = LNT[:, 1, :].rearrange("p (h d) -> p h d", h=4)
        gng = LNT[:, 2, :].rearrange("p (h d) -> p h d", h=4)
        gnb = LNT[:, 3, :].rearrange("p (h d) -> p h d", h=4)

        def normalize(bank, s, b, zs):
            Z, bZ = Zs[zs], bZs[zs]
            P.group(dve, [I("tensor_scalar", out=Z[:, h, :], in0=Z[:, h, :],
                                                          scalar1=ST4[:, s, h:h + 1], scalar2=ST4[:, s, 4 + h:5 + h],
                                                          op0=ALU.subtract, op1=ALU.mult) for h in range(4)],
                    reads=[b], writes=[bZ])

        for m in range(NT):
            bk = next_pj()
            proj_group(slot, m, bk)
            s, b, zs = group_stats(bk, EPSC[:])
            dve.wait([tEPS])
            normalize(bk, s, b, zs)
            chk("vs_f")
            Z, bZ = Zs[zs], bZs[zs]
            P.op(dve, I("tensor_tensor", out=Z, in0=Z, in1=lng, op=ALU.mult), reads=[bZ, bLNT], writes=[bZ])
            P.op(dve, I("tensor_tensor", out=VN[:, m, :].rearrange("p (h d) -> p h d", h=4), in0=Z, in1=lnb,
                                                     op=ALU.add), reads=[bZ, bLNT], writes=[bVN[m]])
        chk("vs")
        slot = load_wblock(w_in_v[:, :, OFF_U + 512 * grp: OFF_U + 512 * grp + 512])
        for m in range(NT):
            i = m % 2
            P.pegroup([I("matmul", bank4(6)[:, h, :], lhsT=WST[:, hs[h], :],
                                                    rhs=VN[:, m, h * 128:(h + 1) * 128], start=True, stop=True)
                       for h in range(4)], reads=[bWST, bVN[m]], writes=[bBANK[6]])
            P.group(act, [I("activation", out=MB[i][:, h, :], in_=bank4(6)[:, h, :], func=AF.Identity,
                                                            bias=TABS[:, 24 + hs[h]:25 + hs[h]], scale=1.0)
                          for h in range(4)], reads=[bBANK[6]], writes=[bMB[i]], extra=[TABTOK])
            bk = next_pj()
            proj_group(slot, m, bk)
            P.op(dve, I("tensor_tensor",
                out=MIXED[:, m, 512 * grp:512 * grp + 512], in0=BANK[bk][:, :],
                in1=MB[i].rearrange("p h d -> p (h d)"), op=ALU.mult), reads=[bBANK[bk], bMB[i]], writes=[bMIX[m]])
        chk("sgu")
        slot = load_wblock(w_in_v[:, :, OFF_G + 512 * grp: OFF_G + 512 * grp + 512])
        for m in range(NT):
            bk = next_pj()
            proj_group(slot, m, bk)
            P.op(act, I("activation", out=G[:, m, :], in_=BANK[bk][:, :], func=AF.Silu),
                 reads=[bBANK[bk]], writes=[bG[m]])
        chk("g")
        slot = load_wblock(w_in_v[:, :, OFF_Q + 512 * grp: OFF_Q + 512 * grp + 512])

        def q_post(m):
            ks = m % 3
            P.pegroup([I("transpose", out=bank_bf(4)[:, h, :], in_=KRB[ks][:, h * 128:(h + 1) * 128],
                                                          identity=IDENT[:]) for h in range(4)],
                      reads=[bKRB[ks]], writes=[bBANK[4]], extra=[TABTOK])
            P.op(act, I("activation", out=QT[:, :, m * 128:(m + 1) * 128], in_=bank_bf(4)[:, 0:4, :],
                                                  func=AF.Copy), reads=[bBANK[4]], writes=[bQT[m]])

        for m in range(NT):
            bk = next_pj()
            proj_group(slot, m, bk)
            rotary(bk, m, m % 2)
            ks = m % 3
            KR32_f, bKR32 = KR32_fs[m % 2], bKR32s[m % 2]
            P.op(act, I("activation", out=KRB[ks], in_=KR32_f, func=AF.Copy), reads=[bKR32], writes=[bKRB[ks]])
            if m >= 2:
                q_post(m - 2)
        q_post(NT - 2)
        q_post(NT - 1)
        chk("q")
        for piece in range(4):
            half = piece % 2
            if nocc:
                P.dma(sp, AGL[half][:, 0].rearrange("p h d -> p (h d)"), ib[grp].ap()[:, :],
                      owner=bAGL[half], reads=[bIB[grp]], writes=[bAGL[half]])
                P.dma(sp, AGL[half][:, 1].rearrange("p h d -> p (h d)"), ib[grp].ap()[:, :],
                      owner=bAGL[half], reads=[bIB[grp]], writes=[bAGL[half]])
            else:
                P.dma(sp, AGL[half].rearrange("p r h d -> p r (h d)"),
                      ob[grp].ap()[piece * 256:(piece + 1) * 256, :].rearrange("(r p) f -> p r f", p=128),
                      owner=bAGL[half], reads=[bOB[grp]], writes=[bAGL[half]])
            for r in range(2):
                cp = piece * 2 + r
                if cp == 0:
                    fns = [I("tensor_scalar",
                        out=RST[:, h, :], in0=AGL[half][:, r, h, :], scalar1=COEF[:, hs[h]:hs[h] + 1], scalar2=None,
                        op0=ALU.mult) for h in range(4)]
                else:
                    fns = [I("scalar_tensor_tensor",
                        out=RST[:, h, :], in0=AGL[half][:, r, h, :], scalar=COEF[:, cp * 8 + hs[h]:cp * 8 + hs[h] + 1],
                        in1=RST[:, h, :], op0=ALU.mult, op1=ALU.add) for h in range(4)]
                P.group(dve, fns, reads=[bAGL[half]], writes=[bRST], extra=[TABTOK])
        chk("comb")
        P.op(act, I("activation", out=RN[0].rearrange("p h d -> p (h d)"), in_=RST_f, func=AF.Copy),
             reads=[bRST], writes=[bRN[0]])
        P.dma(sp, GMT, gm_d[:, 4 * grp:4 * grp + 4, :], owner=bGM, writes=[bGM])
        slot_v1 = load_wblock() if grp == 0 else None
        for n in range(NT):
            i = n % 2
            P.pegroup([I("matmul", bank4(6)[:, h, :], lhsT=KT[:, h, n * 128:(n + 1) * 128],
                                                    rhs=QT[:, h, n * 128:(n + 1) * 128], start=True, stop=True)
                       for h in range(4)], reads=[bKT[n], bQT[n]], writes=[bBANK[6]])
            if n < NT - 1:
                state_update(n)
                P.op(act, I("activation", out=RN[1 - i].rearrange("p h d -> p (h d)"), in_=RST_f, func=AF.Copy),
                     reads=[bRST], writes=[bRN[1 - i]])
            P.op(dve, I("tensor_tensor", out=STB[i], in0=bank4(6), in1=GMT, op=ALU.mult),
                 reads=[bBANK[6], bGM], writes=[bSTB[i]])
            fns = []
            for h in range(4):
                fns.append(I("matmul", bank4(7)[:, h, :], lhsT=STB[i][:, h, :],
                                                             rhs=V[:, n, h * 128:(h + 1) * 128], start=True, stop=False))
                fns.append(I("matmul", bank4(7)[:, h, :], lhsT=QT[:, h, n * 128:(n + 1) * 128],
                                                             rhs=RN[i][:, h, :], start=False, stop=True))
            P.pegroup(fns, reads=[bSTB[i], bV[n], bQT[n], bRN[i]], writes=[bBANK[7]])
            s, b, zs = group_stats(7, TABS[:, 16 + hs[0]:16 + hs[0] + 4])
            normalize(7, s, b, zs)
            Z, Z_f, bZ = Zs[zs], Z_fs[zs], bZs[zs]
            P.op(dve, I("tensor_tensor", out=Z, in0=Z, in1=gng, op=ALU.mult), reads=[bZ, bLNT], writes=[bZ])
            P.op(dve, I("tensor_tensor", out=Z, in0=Z, in1=gnb, op=ALU.add), reads=[bZ, bLNT], writes=[bZ])
            P.op(dve, I("tensor_tensor",
                out=MIXED[:, n, 1024 + 512 * grp:1024 + 512 * grp + 512], in0=Z_f, in1=G[:, n, :], op=ALU.mult),
                reads=[bZ, bG[n]], writes=[bMIX[n]])
            if grp == 0:
                v_proj_tile(slot_v1, n)

    try:
        for grp in range((1 if stop_after == "mix0" else 2) if not done["flag"] else 0):
            mixers(grp)
    except _Stop:
        done["flag"] = True

    if stop_after in ("mix", "mix0"):
        done["flag"] = True

    if not done["flag"]:
        for m in range(NT):
            transposes_to_HT(MIXED[:, m, :], bMIX[m], m)
        phaseA_toks = []
        for b in phaseA_bufs:
            phaseA_toks += b.alltoks()
        for m in range(NT):
            P.dma(sp, X1[:, m, :], x[m * 128:(m + 1) * 128, :], owner=bX1[m], writes=[bX1[m]], extra=phaseA_toks)
        P.dma(sp, GB2, n2g[0:1, :].broadcast_to([128, D]), owner=bGB2, writes=[bGB2], extra=phaseA_toks)
        for cb in range(4):
            slot = load_wblock(w_out_v[:, :, cb * 512:(cb + 1) * 512])
            for m in range(NT):
                bk = next_pj()
                proj_group(slot, m, bk)
                P.op(dve, I("tensor_tensor",
                    out=X1[:, m, cb * 512:(cb + 1) * 512], in0=BANK[bk][:, :], in1=X1[:, m, cb * 512:(cb + 1) * 512],
                    op=ALU.add), reads=[bBANK[bk], bX1[m]], writes=[bX1[m]])
    if stop_after == "oproj":
        done["flag"] = True

    if not done["flag"]:
        for eng_ in (act, dve, pool):
            eng_.wait(phaseA_toks)
        for m in range(NT):
            i = m % 4
            rms_stats(X1[:, m, :], bX1[m], HB2[i], bHB2[i], 8 + m)
            P.op(dve, I("scalar_tensor_tensor", out=HB2[i], in0=X1[:, m, :], scalar=RSTD[:, 8 + m:9 + m],
                                                                 in1=GB2, op0=ALU.mult, op1=ALU.mult),
                 reads=[bX1[m], bSTc[8 + m], bGB2], writes=[bHB2[i]])
            transposes_to_HT(HB2[i], bHB2[i], m)
        ffn_first = list(phaseA_toks)
        for b in bWB:
            ffn_first += b.alltoks()
        nfb = NFB if stop_after != "ffn1" else 1

        def load_gu(q):
            fb, sbk = divmod(q, 2)
            if fb >= nfb:
                return
            slot = q % 2
            f0 = fb * 512 + sbk * 256
            P.dma(pool, WG[slot], w_gate_v[:, :, f0:f0 + 256], owner=bWGU[slot], writes=[bWGU[slot]], extra=ffn_first)
            P.dma(pool, WU[slot], w_up_v[:, :, f0:f0 + 256], owner=bWGU[slot], writes=[], extra=ffn_first)
            t = (bWGU[slot].dsem, 16 * bWGU[slot].dcnt, "d_" + bWGU[slot].name)
            bWGU[slot].add_write(t)
            return slot

        def load_wd(fb):
            if fb >= nfb:
                return
            slot = fb % 2
            P.dma(pool, WD[slot], w_down_v[:, 4 * fb:4 * fb + 4, :], owner=bWD[slot], writes=[bWD[slot]], extra=ffn_first)

        gu_pair = {"n": 0}

        def gu_units(fb):
            for sbk in range(2):
                wslot = (2 * fb + sbk) % 2
                for j in range(2):
                    fc = sbk * 2 + j
                    for hh in range(2):
                        pr = gu_pair["n"] % 2
                        gu_pair["n"] += 1
                        bg, bu = 2 * pr, 2 * pr + 1
                        fns = [I("matmul",
                            BANK[bg][:, :], lhsT=WG[wslot][:, c, j * 128:(j + 1) * 128],
                            rhs=HT[:, c, hh * 512:(hh + 1) * 512], start=(c == 0), stop=(c == DC - 1)) for c in range(DC)]
                        fns += [I("matmul",
                            BANK[bu][:, :], lhsT=WU[wslot][:, c, j * 128:(j + 1) * 128],
                            rhs=HT[:, c, hh * 512:(hh + 1) * 512], start=(c == 0), stop=(c == DC - 1)) for c in range(DC)]
                        P.pegroup(fns, reads=[bWGU[wslot]] + [bHT[4 * hh + k] for k in range(4)],
                                  writes=[bBANK[bg], bBANK[bu]])
                        si = gu_pair["n"] % 2
                        P.op(act, I("activation", out=SG[si], in_=BANK[bg][:, :], func=AF.Silu),
                             reads=[bBANK[bg]], writes=[bSG[si]], extra=phaseA_toks)
                        first = (fc == 0 and hh == 0)
                        t = P.op(dve, I("tensor_tensor",
                            out=AT[fb % 2][:, fc, hh * 512:(hh + 1) * 512], in0=BANK[bu][:, :], in1=SG[si], op=ALU.mult),
                            reads=[bBANK[bu], bSG[si]], writes=[bAT[fb % 2]] if first else [])
                        if not first:
                            bAT[fb % 2].add_write(t)
                load_gu(2 * fb + sbk + 2)

        dn_bank = {"n": 0}

        def dn_units(fb):
            a = fb % 2
            for m in range(NT):
                for cb in range(4):
                    bk = 4 + dn_bank["n"] % 4
                    dn_bank["n"] += 1
                    fns = [I("matmul",
                        BANK[bk][:, :], lhsT=AT[a][:, fc, m * 128:(m + 1) * 128], rhs=WD[a][:, fc, cb * 512:(cb + 1) * 512],
                        start=(fc == 0), stop=(fc == 3)) for fc in range(4)]
                    P.pegroup(fns, reads=[bAT[a], bWD[a]], writes=[bBANK[bk]])
                    P.op(dve, I("tensor_tensor",
                        out=X1[:, m, cb * 512:(cb + 1) * 512], in0=BANK[bk][:, :], in1=X1[:, m, cb * 512:(cb + 1) * 512],
                        op=ALU.add), reads=[bBANK[bk], bX1[m]], writes=[bX1[m]])

        load_gu(0)
        load_gu(1)
        load_wd(0)
        load_wd(1)
        gu_units(0)
        for fb in range(1, nfb):
            gu_units(fb)
            dn_units(fb - 1)
            load_wd(fb + 1)
        dn_units(nfb - 1)
        P.dma(sp, GB2, nfg[0:1, :].broadcast_to([128, D]), owner=bGB2, writes=[bGB2])
        for m in range(NT):
            i = m % 2
            rms_stats(X1[:, m, :], bX1[m], HB2[i], bHB2[i], 16 + m)
            P.op(dve, I("scalar_tensor_tensor", out=X1[:, m, :], in0=X1[:, m, :], scalar=RSTD[:, 16 + m:17 + m],
                                                            in1=GB2, op0=ALU.mult, op1=ALU.mult),
                 reads=[bX1[m], bSTc[16 + m], bGB2], writes=[bX1[m]])
            P.dma(sp, out[m * 128:(m + 1) * 128, :], X1[:, m, :], owner=bX1[m], reads=[bX1[m]])

    for name, getter, shape, dt in dumps:
        dten = nc.dram_tensor("dump_" + name, shape, dt, kind="ExternalOutput").ap()
        ap = getter(locals())
        bd = P.buf("dump_" + name)
        alltoks = []
        for b in P.bufs:
            alltoks += b.alltoks()
        P.dma(sp, dten, ap, owner=bd, extra=alltoks, reads=[bd])

    final = []
    for b in P.bufs:
        final += b.alltoks()
    sp.wait(final)

    with nc.Block() as block:
        for ename, q in (("sync", sp), ("scalar", act), ("vector", dve), ("gpsimd", pool), ("tensor", pe)):
            def body(e, q=q):
                for fn in q.ops:
                    fn(e)
            getattr(block, ename)(body)
    es.close()
    return nc


def make_in_maps(x, norm1_g, w_in, sgu_ln_g, sgu_ln_b, w_spatial, b_spatial, ret_gn_g, ret_gn_b,
                 w_out, norm2_g, w_gate, w_up, w_down, final_norm_g):
    f32 = lambda a: np.ascontiguousarray(np.asarray(a), dtype=np.float32)
    zeta, gneg, eps2, _, mask, ident = _const_tables()
    tabs = np.concatenate([zeta, gneg, eps2, f32(b_spatial)[0].T.astype(np.float64)], axis=1).astype(np.float32)
    gm = (gneg[:, :, None] * mask.astype(np.float64)[:, None, :]).astype(np.float32)
    shared = {
        "w_in": f32(w_in)[0], "w_out": f32(w_out)[0], "w_gate": f32(w_gate)[0], "w_up": f32(w_up)[0],
        "w_down": f32(w_down)[0],
        "n1g": f32(norm1_g).reshape(1, D), "n2g": f32(norm2_g).reshape(1, D), "nfg": f32(final_norm_g).reshape(1, D),
        "lnt": np.stack([f32(sgu_ln_g).reshape(-1), f32(sgu_ln_b).reshape(-1), f32(ret_gn_g).reshape(-1),
                         f32(ret_gn_b).reshape(-1)], axis=0),
        "wsp": f32(w_spatial)[0],
        "tabs": np.ascontiguousarray(tabs), "gm": np.ascontiguousarray(gm),
        "ident": ident.astype(ml_dtypes.bfloat16), "mask": mask.astype(ml_dtypes.bfloat16),
    }
    xf = f32(x)[0]
    in_maps = []
    for c in range(NCORES):
        cos, sin, coef = _core_tables(c)
        m = dict(shared)
        m["x"] = np.ascontiguousarray(xf[c * TPC:(c + 1) * TPC])
        m["cos"], m["sin"], m["coef"] = cos, sin, coef
        in_maps.append(m)
    return in_maps


_NC_CACHE = {}


def kernel(**inputs):
    in_maps = make_in_maps(**inputs)
    if "nc" not in _NC_CACHE:
        _NC_CACHE["nc"] = build_program()
    res = run_bass_kernel_spmd(_NC_CACHE["nc"], in_maps, core_ids=list(range(NCORES)))
    return np.concatenate([r["out"] for r in res.results], axis=0).reshape(1, SEQ, D).astype(np.float32)
```
